# Optimizing a Trainium2 kernel written in Bass

```python
import jax, jax.numpy as jnp
from jax import lax
import numpy as np

D_MODEL = 2048
BATCH = 4
SEQ = 4096
DEPTH = 2

N_HEADS = 16
N_KV_HEADS = 4
HEAD_DIM = 128
ATTN_WIDTH = N_HEADS * HEAD_DIM
KV_WIDTH = N_KV_HEADS * HEAD_DIM
WINDOW = 128
BLOCK = 128
LRU_WIDTH = D_MODEL
LRU_BLOCKS = 16
LRU_BLOCK_W = LRU_WIDTH // LRU_BLOCKS
LRU_CONV = 4
LRU_C = 8.0
CONV_WIDTH = D_MODEL
CONV_KERNEL = 31
N_BRANCH = 3
MEM_LEN = 256
XATTN_HEADS = 4
XATTN_HEAD_DIM = 128
XATTN_WIDTH = XATTN_HEADS * XATTN_HEAD_DIM
FF_HIDDEN = -(-(8 * D_MODEL) // (3 * 256)) * 256
EPS = 1e-6
NEG_INF = -1e30

N_IN = ATTN_WIDTH + 2 * KV_WIDTH + 2 * LRU_WIDTH + 2 * CONV_WIDTH + N_BRANCH * D_MODEL
SPLITS = list(np.cumsum([ATTN_WIDTH, KV_WIDTH, KV_WIDTH, LRU_WIDTH, LRU_WIDTH, 2 * CONV_WIDTH]))

kernel_name = "hybrid_gqa_rglru_conformer_encoder"


def rmsnorm(x, g):
    x32 = x.astype(jnp.float32)
    y = x32 * lax.rsqrt(jnp.mean(x32 * x32, axis=-1, keepdims=True) + EPS)
    return (y * g.astype(jnp.float32)).astype(x.dtype)


def layernorm(x, g, b):
    x32 = x.astype(jnp.float32)
    mu = jnp.mean(x32, axis=-1, keepdims=True)
    var = jnp.mean(jnp.square(x32 - mu), axis=-1, keepdims=True)
    y = (x32 - mu) * lax.rsqrt(var + EPS)
    return (y * g.astype(jnp.float32) + b.astype(jnp.float32)).astype(x.dtype)


def depthwise_conv(x, w, b, pad):
    C = x.shape[-1]
    y = lax.conv_general_dilated(x, w[:, None, :].astype(x.dtype), window_strides=(1,), padding=[pad],
                                 dimension_numbers=('NWC', 'WIO', 'NWC'), feature_group_count=C)
    return y + b


def alibi_slopes(n_heads):
    return jnp.exp2(-(8.0 / n_heads) * jnp.arange(1, n_heads + 1, dtype=jnp.float32))


def windowed_gqa(q, k, v, sink):
    B, S = q.shape[0], q.shape[1]
    nb = S // BLOCK
    grp = N_HEADS // N_KV_HEADS
    qb = q.reshape(B, nb, BLOCK, N_KV_HEADS, grp, HEAD_DIM)

    def bands(t):
        tp = jnp.pad(t, ((0, 0), (BLOCK, BLOCK), (0, 0), (0, 0)))
        tp = tp.reshape(B, nb + 2, BLOCK, N_KV_HEADS, HEAD_DIM)
        return jnp.concatenate([tp[:, :-2], tp[:, 1:-1], tp[:, 2:]], axis=2)

    kb, vb = bands(k), bands(v)
    s = jnp.einsum('bnqkgd,bnskd->bnkgqs', qb, kb,
                   preferred_element_type=jnp.float32) * (HEAD_DIM ** -0.5)
    qi = jnp.arange(BLOCK)
    kj = jnp.arange(3 * BLOCK)
    rel = qi[:, None] + BLOCK - kj[None, :]
    kpos = jnp.arange(nb)[:, None] * BLOCK - BLOCK + kj[None, :]
    valid = (jnp.abs(rel) <= WINDOW)[None] & ((kpos >= 0) & (kpos < S))[:, None, :]
    slopes = alibi_slopes(N_HEADS).reshape(N_KV_HEADS, grp)
    bias = -slopes[:, :, None, None] * jnp.abs(rel).astype(jnp.float32)[None, None]
    s = jnp.where(valid[None, :, None, None], s + bias, NEG_INF)
    sink_col = jnp.broadcast_to(sink.astype(jnp.float32).reshape(1, 1, N_KV_HEADS, grp, 1, 1),
                                s.shape[:-1] + (1,))
    p = jax.nn.softmax(jnp.concatenate([s, sink_col], axis=-1), axis=-1)[..., :-1]
    o = jnp.einsum('bnkgqs,bnskd->bnqkgd', p.astype(v.dtype), vb)
    return o.reshape(B, S, ATTN_WIDTH)


def _lin_combine(left, right):
    a1, b1 = left
    a2, b2 = right
    return a1 * a2, a2 * b1 + b2


def rglru_direction(xl, conv_w, conv_b, wr, br, wi, bi, lam, reverse):
    B, S, C = xl.shape
    pad = (0, LRU_CONV - 1) if reverse else (LRU_CONV - 1, 0)
    xc = depthwise_conv(xl, conv_w, conv_b, pad)
    xb = xc.reshape(B, S, LRU_BLOCKS, LRU_BLOCK_W)
    r = jax.nn.sigmoid(jnp.einsum('bsnc,ncd->bsnd', xb, wr).reshape(B, S, C) + br)
    i = jax.nn.sigmoid(jnp.einsum('bsnc,ncd->bsnd', xb, wi).reshape(B, S, C) + bi)
    log_a = -LRU_C * r.astype(jnp.float32) * jax.nn.softplus(-lam.astype(jnp.float32))
    a = jnp.exp(log_a)
    b = jnp.sqrt(-jnp.expm1(2.0 * log_a)) * (i * xc).astype(jnp.float32)
    _, h = lax.associative_scan(_lin_combine, (a, b), reverse=reverse, axis=1)
    return h.astype(xl.dtype)


def conformer_conv(u, dw_w, dw_b, ln_g, ln_b):
    u1, u2 = jnp.split(u, 2, axis=-1)
    glu = u1 * jax.nn.sigmoid(u2)
    half = (CONV_KERNEL - 1) // 2
    c = depthwise_conv(glu, dw_w, dw_b, (half, half))
    return jax.nn.silu(layernorm(c, ln_g, ln_b))


def memory_cross_attention(h, mem_n, wq, wkv, wo):
    B, S, _ = h.shape
    M = mem_n.shape[1]
    q = (h @ wq).reshape(B, S, XATTN_HEADS, XATTN_HEAD_DIM)
    k, v = jnp.split(mem_n @ wkv, 2, axis=-1)
    k = k.reshape(B, M, XATTN_HEADS, XATTN_HEAD_DIM)
    v = v.reshape(B, M, XATTN_HEADS, XATTN_HEAD_DIM)
    s = jnp.einsum('bshd,bmhd->bhsm', q, k, preferred_element_type=jnp.float32) * (XATTN_HEAD_DIM ** -0.5)
    p = jax.nn.softmax(s, axis=-1)
    o = jnp.einsum('bhsm,bmhd->bshd', p.astype(v.dtype), v).reshape(B, S, XATTN_WIDTH)
    return o @ wo


def setup_inputs(seed: int = 0) -> dict:
    key = jax.random.key(seed)
    ks = iter(jax.random.split(key, 40))
    f32 = jnp.float32

    def nrm(shape, scale):
        return jax.random.normal(next(ks), shape, f32) * scale

    def gain(shape):
        return 1.0 + nrm(shape, 0.02)

    u = jax.random.uniform(next(ks), (DEPTH, 2, LRU_WIDTH), f32, minval=0.9, maxval=0.999)
    s_base = u ** (1.0 / LRU_C)
    lru_lambda = jnp.log(s_base) - jnp.log1p(-s_base)
    return {
        "x": nrm((BATCH, SEQ, D_MODEL), 1.0),
        "mem": nrm((BATCH, MEM_LEN, D_MODEL), 1.0),
        "norm_mix": gain((DEPTH, D_MODEL)),
        "w_in": nrm((DEPTH, D_MODEL, N_IN), D_MODEL ** -0.5),
        "gate_bias": nrm((DEPTH, N_BRANCH * D_MODEL), 0.01),
        "attn_sink": nrm((DEPTH, N_HEADS), 0.5),
        "lru_conv_w": nrm((DEPTH, 2, LRU_CONV, LRU_WIDTH), LRU_CONV ** -0.5),
        "lru_conv_b": nrm((DEPTH, 2, LRU_WIDTH), 0.01),
        "lru_wr": nrm((DEPTH, 2, LRU_BLOCKS, LRU_BLOCK_W, LRU_BLOCK_W), LRU_BLOCK_W ** -0.5),
        "lru_br": nrm((DEPTH, 2, LRU_WIDTH), 0.01),
        "lru_wi": nrm((DEPTH, 2, LRU_BLOCKS, LRU_BLOCK_W, LRU_BLOCK_W), LRU_BLOCK_W ** -0.5),
        "lru_bi": nrm((DEPTH, 2, LRU_WIDTH), 0.01),
        "lru_lambda": lru_lambda,
        "conv_dw_w": nrm((DEPTH, CONV_KERNEL, CONV_WIDTH), CONV_KERNEL ** -0.5),
        "conv_dw_b": nrm((DEPTH, CONV_WIDTH), 0.01),
        "conv_ln_g": gain((DEPTH, CONV_WIDTH)),
        "conv_ln_b": nrm((DEPTH, CONV_WIDTH), 0.01),
        "w_proj_attn": nrm((DEPTH, ATTN_WIDTH, D_MODEL), ATTN_WIDTH ** -0.5),
        "w_proj_lru": nrm((DEPTH, LRU_WIDTH, D_MODEL), LRU_WIDTH ** -0.5),
        "w_proj_conv": nrm((DEPTH, CONV_WIDTH, D_MODEL), CONV_WIDTH ** -0.5),
        "w_out": nrm((DEPTH, D_MODEL, D_MODEL), D_MODEL ** -0.5),
        "norm_cross": gain((DEPTH, D_MODEL)),
        "norm_mem": gain((DEPTH, D_MODEL)),
        "xattn_wq": nrm((DEPTH, D_MODEL, XATTN_WIDTH), D_MODEL ** -0.5),
        "xattn_wkv": nrm((DEPTH, D_MODEL, 2 * XATTN_WIDTH), D_MODEL ** -0.5),
        "xattn_wo": nrm((DEPTH, XATTN_WIDTH, D_MODEL), XATTN_WIDTH ** -0.5),
        "norm_ffn": gain((DEPTH, D_MODEL)),
        "ffn_w13": nrm((DEPTH, D_MODEL, 2 * FF_HIDDEN), D_MODEL ** -0.5),
        "ffn_w2": nrm((DEPTH, FF_HIDDEN, D_MODEL), FF_HIDDEN ** -0.5),
        "norm_final": gain((D_MODEL,)),
    }


def reference(x, mem, norm_mix, w_in, gate_bias, attn_sink, lru_conv_w, lru_conv_b, lru_wr, lru_br,
              lru_wi, lru_bi, lru_lambda, conv_dw_w, conv_dw_b, conv_ln_g, conv_ln_b, w_proj_attn,
              w_proj_lru, w_proj_conv, w_out, norm_cross, norm_mem, xattn_wq, xattn_wkv, xattn_wo,
              norm_ffn, ffn_w13, ffn_w2, norm_final):
    B, S, _ = x.shape
    for l in range(DEPTH):
        h = rmsnorm(x, norm_mix[l])
        z = h @ w_in[l]
        q, k, v, xl, gl, u, gates = jnp.split(z, SPLITS, axis=-1)
        ya = windowed_gqa(q.reshape(B, S, N_HEADS, HEAD_DIM),
                          k.reshape(B, S, N_KV_HEADS, HEAD_DIM),
                          v.reshape(B, S, N_KV_HEADS, HEAD_DIM), attn_sink[l])
        h_fwd = rglru_direction(xl, lru_conv_w[l, 0], lru_conv_b[l, 0], lru_wr[l, 0], lru_br[l, 0],
                                lru_wi[l, 0], lru_bi[l, 0], lru_lambda[l, 0], reverse=False)
        h_bwd = rglru_direction(xl, lru_conv_w[l, 1], lru_conv_b[l, 1], lru_wr[l, 1], lru_br[l, 1],
                                lru_wi[l, 1], lru_bi[l, 1], lru_lambda[l, 1], reverse=True)
        yl = (h_fwd + h_bwd) * jax.nn.gelu(gl)
        yc = conformer_conv(u, conv_dw_w[l], conv_dw_b[l], conv_ln_g[l], conv_ln_b[l])
        g = jax.nn.sigmoid(gates + gate_bias[l]).reshape(B, S, N_BRANCH, D_MODEL)
        merged = (g[:, :, 0] * (ya @ w_proj_attn[l])
                  + g[:, :, 1] * (yl @ w_proj_lru[l])
                  + g[:, :, 2] * (yc @ w_proj_conv[l]))
        x = x + merged @ w_out[l]
        hc = rmsnorm(x, norm_cross[l])
        mem_n = rmsnorm(mem, norm_mem[l])
        x = x + memory_cross_attention(hc, mem_n, xattn_wq[l], xattn_wkv[l], xattn_wo[l])
        hf = rmsnorm(x, norm_ffn[l])
        a1, a3 = jnp.split(hf @ ffn_w13[l], 2, axis=-1)
        x = x + (jax.nn.silu(a1) * a3) @ ffn_w2[l]
    return rmsnorm(x, norm_final)
```

```python
import os
from contextlib import ExitStack
import numpy as np
import concourse.bass as bass
import concourse.mybir as mybir
from concourse.bass_utils import run_bass_kernel_spmd

F32 = mybir.dt.float32
BF16 = mybir.dt.bfloat16
AF = mybir.ActivationFunctionType
ALU = mybir.AluOpType

NCORES = 8
DEPTH = 2
D = 2048
KC = 16
T = 2048
H = 128
E = T + 2 * H
SEQ = 4096
NIN = 17408
FFH = 5632
EPS = 1e-6
LRU_C = 8.0
CK = 31
CQ, CK_, CV, CXL, CGL, CU1, CU2, CG = 0, 16, 20, 24, 40, 56, 72, 88
PV_OFF = {}
_o = 0
for _n, _w in (("norm_mix", 16), ("norm_cross", 16), ("norm_ffn", 16), ("norm_mem", 16), ("gate_bias", 48),
               ("lcw", 128), ("lcb", 32), ("lbr", 32), ("lbi", 32), ("lam", 32), ("dww", 496), ("dwb", 16),
               ("lng", 16), ("lnb", 16)):
    PV_OFF[_n] = _o
    _o += _w
PV_N = _o
WSHARD = os.environ.get("MK_WSHARD", "1") == "1"
WSPEC = [("w_in_t", [136, 128, 16, 128], 0), ("lrw_t", [128, 2, 2, 16, 128], 1),
         ("wp_t", [3, 16, 128, 16, 128], 1), ("wout_t", [16, 128, 16, 128], 1),
         ("xq_t", [4, 128, 16, 128], 1), ("xkv_t", [8, 128, 16, 128], 1),
         ("xo_t", [16, 128, 4, 128], 1), ("w13_t", [88, 128, 16, 128], 2),
         ("w2_t", [16, 128, 44, 128], 2)]
WMAP = {}
PIECE_ROWS = [0, 0, 0]
for _n, _sh, _g in WSPEC:
    _rows = int(np.prod(_sh)) // 128
    WMAP[_n] = (PIECE_ROWS[_g], _sh, _rows, _g)
    PIECE_ROWS[_g] += _rows
assert all(r % 8 == 0 for r in PIECE_ROWS)
WR8 = sum(PIECE_ROWS) * DEPTH // 8
DEBUG = os.environ.get("MK_DEBUG", "")
STOP = os.environ.get("MK_STOP", "")
DBG_L = int(os.environ.get("MK_DBG_LAYER", "0"))


class Tile:
    def __init__(self, fw, t, name, stack):
        self.fw, self.t, self.name, self.stack = fw, t, name, stack
        self.sem = None
        self.last_dma = None

    def __getitem__(self, idx):
        return self.t[idx]

    def get_sem(self):
        if self.sem is None:
            self.sem = self.fw.sem_pool.pop()
            self.stack.callback(self._release)
        return self.sem

    def _release(self):
        self.fw.sem_pool.append(self.sem)
        self.sem = None


class FW:
    ENGS = ("pe", "act", "dve", "pool", "sp")

    def __init__(self, nc, stack):
        self.nc, self.stack = nc, stack
        self.eng = {"pe": nc.tensor, "act": nc.scalar, "dve": nc.vector, "pool": nc.gpsimd, "sp": nc.sync}
        self.esem = {e: stack.enter_context(nc.semaphore("e_" + e)) for e in self.ENGS}
        self.ecount = {e: 0 for e in self.ENGS}
        self.waited = {e: {} for e in self.ENGS}
        self.res = {}
        self.dirty = {}
        self.uid = 0
        self.n_inst = 0
        self.sem_pool = [stack.enter_context(nc.semaphore(f"dq{i}")) for i in range(72)]
        self.semcount = {}

    def sbuf(self, stack, name, shape, dtype):
        self.uid += 1
        nm = f"{name}_{self.uid}"
        return Tile(self, stack.enter_context(self.nc.sbuf_tensor(nm, list(shape), dtype)), nm, stack)

    def psum(self, stack, name, shape, dtype=F32):
        self.uid += 1
        nm = f"{name}_{self.uid}"
        return Tile(self, stack.enter_context(self.nc.psum_tensor(nm, list(shape), dtype)), nm, stack)

    def _deps(self, reads, writes):
        evs = []
        for r in reads:
            st = self.res.get(r)
            if st and st["w"] is not None:
                evs.append(st["w"])
        for w in writes:
            st = self.res.get(w)
            if st:
                if st["w"] is not None:
                    evs.append(st["w"])
                evs.extend(st["r"].values())
        return evs

    def _wait(self, e, evs):
        wd = self.waited[e]
        best = {}
        for sem, val in evs:
            k = sem.name
            if e == "pe" and k == self.esem["pe"].name:
                continue
            if wd.get(k, 0) >= val:
                continue
            if k not in best or best[k][1] < val:
                best[k] = (sem, val)
        for k, (sem, val) in best.items():
            self.eng[e].wait_ge(sem, val)
            wd[k] = val
            self.n_inst += 1

    def _record(self, ev, reads, writes):
        for r in reads:
            st = self.res.setdefault(r, {"w": None, "r": {}})
            st["r"][ev[0].name] = ev
        for w in writes:
            self.res[w] = {"w": ev, "r": {}}

    def op(self, e, fn, reads=(), writes=()):
        self._wait(e, self._deps(reads, writes))
        inst = fn(self.eng[e])
        self.n_inst += 1
        self.ecount[e] += 1
        inst.then_inc(self.esem[e], 1)
        ev = (self.esem[e], self.ecount[e])
        self._record(ev, reads, writes)
        return ev

    def dma(self, q, out, in_, tile, reads=(), writes=()):
        evs = self._deps(reads, writes)
        if tile.last_dma is not None:
            evs.append(tile.last_dma)
        self._wait(q, evs)
        sem = tile.get_sem()
        inst = self.eng[q].dma_start(out=out, in_=in_)
        self.n_inst += 1
        cnt = self.semcount.get(sem.name, 0) + 16
        self.semcount[sem.name] = cnt
        inst.then_inc(sem, 16)
        ev = (sem, cnt)
        tile.last_dma = ev
        self.dirty[sem.name] = ev
        self._record(ev, reads, writes)
        return ev

    def barrier(self):
        evs = [(self.esem[e], self.ecount[e]) for e in self.ENGS if self.ecount[e] > 0]
        evs += list(self.dirty.values())
        for e in self.ENGS:
            self._wait(e, evs)
        self.res = {}
        self.dirty = {}


def build_program():
    nc = bass.Bass("TRN2", target_bir_lowering=False)

    def din(name, shape):
        return nc.dram_tensor(name, list(shape), F32, kind="ExternalInput")

    xin = din("xin", [KC, 128, E])
    memT = din("memT", [KC, 128, 256])
    vmask_d = din("vmask", [128, 2])
    abias_d = din("abias", [128, 5, 4, 512])
    ident_d = din("ident", [128, 128])
    pv_d = din("pv", [DEPTH, 128, PV_N])
    nf_d = din("nf", [128, 16])
    sink_d = din("sink", [DEPTH, 1, 16])
    class _V:
        def __init__(self, a):
            self.a = a

        def ap(self):
            return self.a

    class _LV:
        def __init__(self, aps):
            self.aps = aps

        def __getitem__(self, idx):
            if not isinstance(idx, tuple):
                idx = (idx,)
            a = self.aps[idx[0]]
            return a[idx[1:]] if len(idx) > 1 else a

    if not WSHARD:
        _wt = {n: din(n, [DEPTH] + sh) for n, sh, _ in WSPEC}
    else:
        wsh = din("wsh", [WR8, 128])
        wbn = nc.dram_tensor("wbn", [WR8, 128], F32)
        pieces = [[nc.dram_tensor(f"wfull_{l_}_{g_}", [PIECE_ROWS[g_], 128], F32) for g_ in range(3)]
                  for l_ in range(DEPTH)]
        _wt = {}
        for n, sh, g_ in WSPEC:
            r0, _, rows, _ = WMAP[n]
            names = [f"a{i}" for i in range(len(sh) - 1)]
            pat = "(" + " ".join(names) + ") n -> " + " ".join(names) + " n"
            kw = {names[i]: sh[i] for i in range(len(sh) - 1)}
            _wt[n] = _V(_LV([pieces[l_][g_].ap()[r0:r0 + rows, :].rearrange(pat, **kw) for l_ in range(DEPTH)]))
    w_in_t, lrw_t, wp_t, wout_t = _wt["w_in_t"], _wt["lrw_t"], _wt["wp_t"], _wt["wout_t"]
    xq_t, xkv_t, xo_t, w13_t, w2_t = _wt["xq_t"], _wt["xkv_t"], _wt["xo_t"], _wt["w13_t"], _wt["w2_t"]
    yout = nc.dram_tensor("yout", [KC, 128, T], F32, kind="ExternalOutput")
    dbg = {}
    if DEBUG:
        for nm in DEBUG.split(","):
            dbg[nm] = nc.dram_tensor("dbg_" + nm, [KC, 128, T], F32, kind="ExternalOutput")

    def dscr(name, shape, dt=F32):
        return nc.dram_tensor(name, list(shape), dt)

    xres = dscr("xres", [KC, 128, T])
    ggT = dscr("ggT", [16, 128, T], BF16)
    aT = dscr("aT", [2, 16, 128, T])
    bT = dscr("bT", [2, 16, 128, T])
    yaT = dscr("yaT", [16, 128, T], BF16)
    ylT = dscr("ylT", [16, 128, T], BF16)
    ycT = dscr("ycT", [16, 128, T], BF16)
    gT = dscr("gT", [48, 128, T], BF16)
    cT = dscr("cT", [16, 128, T])
    sx_in = dscr("sx_in", [128, 32])
    sx_out = dscr("sx_out", [256, 32])
    hx_in = dscr("hx_in", [128, KC * 2 * 128])
    hx_out = dscr("hx_out", [256, KC * 2 * 128])

    with ExitStack() as stack:
        fw = FW(nc, stack)
        cc_sem = stack.enter_context(nc.semaphore("cc_sem"))
        cc_count = [0]
        ps = fw.psum(stack, "ps", [128, 4096], F32)
        ident = fw.sbuf(stack, "ident", [128, 128], BF16)
        ones = fw.sbuf(stack, "ones", [128, 128], BF16)
        zrow = fw.sbuf(stack, "zrow", [1, 512], F32)
        vmask = fw.sbuf(stack, "vmask", [128, 2], F32)
        pv = fw.sbuf(stack, "pv", [128, PV_N], F32)
        pc = fw.sbuf(stack, "pc", [128, 192], F32)
        nf = fw.sbuf(stack, "nf", [128, 16], F32)
        sinkt = fw.sbuf(stack, "sinkt", [1, 16], F32)
        esink = fw.sbuf(stack, "esink", [1, 2048], BF16)
        PC_HBR, PC_HBI, PC_C, PC_HC = 0, 32, 64, 96

        def bank(b, n=512):
            return ps[:, b * 512:b * 512 + n]

        def PB(b):
            return ("ps", b)

        fw.dma("pool", ident[:], ident_d.ap(), ident, writes=[ident])
        fw.dma("sp", vmask[:], vmask_d.ap(), vmask, writes=[vmask])
        fw.dma("sp", nf[:], nf_d.ap(), nf, writes=[nf])
        fw.op("dve", lambda e: e.memset(ones[:], 1.0), writes=[ones])
        fw.op("dve", lambda e: e.memset(zrow[:], 0.0), writes=[zrow])
        fw.barrier()
        if WSHARD:
            CH = 4096
            for r0 in range(0, WR8, CH):
                r1 = min(WR8, r0 + CH)
                fw.dma("pool", wbn.ap()[r0:r1, :], wsh.ap()[r0:r1, :], zrow)
            fw.barrier()
            off = 0
            for l_ in range(DEPTH):
                for g_ in range(3):
                    r8 = PIECE_ROWS[g_] // 8
                    nc.gpsimd.collective_compute("AllGather", ALU.bypass,
                                                 replica_groups=[list(range(8))],
                                                 ins=[wbn.ap()[off:off + r8, :]],
                                                 outs=[pieces[l_][g_].ap()]).then_inc(cc_sem)
                    cc_count[0] += 1
                    off += r8
            for e_ in fw.ENGS:
                fw.eng[e_].wait_ge(cc_sem, cc_count[0])

        def dbg_store(nm, src_tile, src_ap, kc, t0, n, reads):
            if nm in dbg:
                fw.dma("sp", dbg[nm].ap()[kc, :, t0:t0 + n], src_ap, src_tile, reads=reads)

        def collective(in_h, out_h):
            g = nc.gpsimd
            fw.barrier()
            g.collective_compute("AllGather", ALU.bypass,
                                 replica_groups=[[0, 1], [2, 3], [4, 5], [6, 7]],
                                 ins=[in_h.ap().opt()], outs=[out_h.ap().opt()]).then_inc(cc_sem)
            cc_count[0] += 1
            for e in fw.ENGS:
                fw.eng[e].wait_ge(cc_sem, cc_count[0])

        def rmsnorm_phase(st, l, src_kind, gcol, out_tile=None, out_dram=None, with_halo=False, gtile=None):
            gt = gtile if gtile is not None else pv
            with ExitStack() as ph:
                xt = [fw.sbuf(ph, "xt", [128, KC, 512], F32) for _ in range(2)]
                sq = [fw.sbuf(ph, "sq", [128, 512], BF16) for _ in range(3)]
                lnv = fw.sbuf(ph, "lnv", [128, 512], F32)
                rstd = [fw.sbuf(ph, "rstd", [128, 512], F32) for _ in range(2)]
                ot = [fw.sbuf(ph, "ot", [128, KC, 512], F32) for _ in range(2)] if out_dram is not None else None
                tiles = [("loc", tt * 512, 512) for tt in range(4)]
                if with_halo:
                    tiles.append(("halo", 0, 256))
                sqi = 0
                for ti, (kind, t0, n) in enumerate(tiles):
                    b = ti % 2
                    x_t = xt[b]
                    if kind == "loc":
                        if src_kind == "xin":
                            src = xin.ap().rearrange("kc p t -> p kc t")[:, :, H + t0:H + t0 + n]
                        else:
                            src = xres.ap().rearrange("kc p t -> p kc t")[:, :, t0:t0 + n]
                        fw.dma("sp", x_t[:, :, 0:n], src, x_t, writes=[x_t])
                    else:
                        if src_kind == "xin":
                            fw.dma("sp", x_t[:, :, 0:128], xin.ap().rearrange("kc p t -> p kc t")[:, :, 0:128],
                                   x_t, writes=[x_t])
                            fw.dma("sp", x_t[:, :, 128:256],
                                   xin.ap().rearrange("kc p t -> p kc t")[:, :, E - 128:E], x_t, writes=[x_t])
                        else:
                            hv = hx_out.ap().rearrange("(r p) (kc s t) -> r p kc s t", p=128, kc=KC, s=2)
                            fw.dma("pool", x_t[:, :, 0:128], hv[0, :, :, 1, :], x_t, writes=[x_t])
                            fw.dma("pool", x_t[:, :, 128:256], hv[1, :, :, 0, :], x_t, writes=[x_t])
                        fw.op("dve", lambda e: e.tensor_scalar(x_t[:, :, 0:128], x_t[:, :, 0:128], vmask[:, 0:1], None,
                                                               ALU.mult), reads=[x_t, vmask], writes=[x_t])
                        fw.op("dve", lambda e: e.tensor_scalar(x_t[:, :, 128:256], x_t[:, :, 128:256], vmask[:, 1:2],
                                                               None, ALU.mult), reads=[x_t, vmask], writes=[x_t])
                    pb = 6 + (ti % 2)
                    for kc in range(KC):
                        s_t = sq[sqi % 3]
                        sqi += 1
                        fw.op("act", lambda e: e.activation(s_t[:, 0:n], x_t[:, kc, 0:n], AF.Square),
                              reads=[x_t], writes=[s_t])
                        fw.op("pe", lambda e: e.matmul(bank(pb, n), ones[:], s_t[:, 0:n], start=(kc == 0),
                                                       stop=(kc == KC - 1)), reads=[s_t, ones], writes=[PB(pb)])
                    r_t = rstd[b]
                    fw.op("act", lambda e: e.activation(lnv[:, 0:n], bank(pb, n), AF.Ln, bias=EPS, scale=1.0 / D),
                          reads=[PB(pb)], writes=[lnv])
                    fw.op("act", lambda e: e.activation(r_t[:, 0:n], lnv[:, 0:n], AF.Exp, scale=-0.5),
                          reads=[lnv], writes=[r_t])
                    for kc in range(KC):
                        if out_tile is not None:
                            if kind == "loc":
                                e0 = (H + t0) if with_halo else t0
                                dsts = [(out_tile[:, kc, e0:e0 + n], 0, n)]
                            else:
                                dsts = [(out_tile[:, kc, 0:128], 0, 128), (out_tile[:, kc, E - 128:E], 128, 128)]
                            for dst, c0, cn in dsts:
                                fw.op("dve", lambda e: e.scalar_tensor_tensor(
                                    dst, x_t[:, kc, c0:c0 + cn], gt[:, gcol + kc:gcol + kc + 1], r_t[:, c0:c0 + cn],
                                    ALU.mult, ALU.mult), reads=[x_t, r_t, gt], writes=[(out_tile, ti)])
                        else:
                            o_t = ot[b]
                            fw.op("dve", lambda e: e.scalar_tensor_tensor(
                                o_t[:, kc, 0:n], x_t[:, kc, 0:n], gt[:, gcol + kc:gcol + kc + 1], r_t[:, 0:n],
                                ALU.mult, ALU.mult), reads=[x_t, r_t, gt], writes=[o_t])
                    if out_dram is not None:
                        fw.dma("sp", out_dram.ap().rearrange("kc p t -> p kc t")[:, :, t0:t0 + n], ot[b][:, :, 0:n],
                               ot[b], reads=[ot[b]])
                fw.barrier()

        class WStream:
            def __init__(self, st, shape, nbuf=3):
                self.bufs = [fw.sbuf(st, "wb", shape, BF16) for _ in range(nbuf)]
                self.i = 0

            def load(self, src_ap, sub=None):
                t = self.bufs[self.i % len(self.bufs)]
                self.i += 1
                dst = t[:] if sub is None else sub(t)
                fw.dma("pool", dst, src_ap, t, writes=[t])
                return t

        def run_jobs(jobs, ws, depth=2):
            loaded = {}

            def ld(i):
                loaded[i] = [ws.load(s) for s in jobs[i][0]]

            def pre(i):
                if len(jobs[i]) > 2 and jobs[i][2] is not None:
                    jobs[i][2]()

            for i in range(min(depth, len(jobs))):
                ld(i)
            if jobs:
                pre(0)
            for i in range(len(jobs)):
                if i + 1 < len(jobs):
                    pre(i + 1)
                jobs[i][1](loaded.pop(i))
                if i + depth < len(jobs):
                    ld(i + depth)

        for l in range(DEPTH):
            fw.dma("sp", pv[:], pv_d.ap()[l], pv, writes=[pv])
            fw.dma("sp", sinkt[:], sink_d.ap()[l], sinkt, writes=[sinkt])
            o = PV_OFF
            fw.op("dve", lambda e: e.tensor_scalar(pc[:, PC_HBR:PC_HBR + 32], pv[:, o["lbr"]:o["lbr"] + 32], 0.5, None,
                                                   ALU.mult), reads=[pv], writes=[pc])
            fw.op("dve", lambda e: e.tensor_scalar(pc[:, PC_HBI:PC_HBI + 32], pv[:, o["lbi"]:o["lbi"] + 32], 0.5, None,
                                                   ALU.mult), reads=[pv], writes=[pc])
            fw.op("act", lambda e: e.activation(pc[:, PC_C:PC_C + 32], pv[:, o["lam"]:o["lam"] + 32], AF.Exp, scale=-1.0),
                  reads=[pv, pc], writes=[pc])
            fw.op("act", lambda e: e.activation(pc[:, PC_C:PC_C + 32], pc[:, PC_C:PC_C + 32], AF.Ln, bias=1.0, scale=1.0),
                  reads=[pc], writes=[pc])
            fw.op("dve", lambda e: e.tensor_scalar(pc[:, PC_C:PC_C + 32], pc[:, PC_C:PC_C + 32], -LRU_C, None, ALU.mult),
                  reads=[pc], writes=[pc])
            fw.op("dve", lambda e: e.tensor_scalar(pc[:, PC_HC:PC_HC + 32], pc[:, PC_C:PC_C + 32], 0.5, None, ALU.mult),
                  reads=[pc], writes=[pc])
            for h in range(16):
                fw.op("act", lambda e: e.activation(esink[0:1, h * 128:(h + 1) * 128], zrow[0:1, 0:128], AF.Exp,
                                                    bias=sinkt[0:1, h:h + 1], scale=0.0),
                      reads=[sinkt, zrow], writes=[esink])
            fw.barrier()

            with ExitStack() as L1:
                hT = fw.sbuf(L1, "hT", [128, KC, E], BF16)
                rmsnorm_phase(L1, l, "xin" if l == 0 else "xres", PV_OFF["norm_mix"], out_tile=hT, with_halo=True)

                etiles = [(0, 128)] + [(H + tt * 512, 512) for tt in range(4)] + [(E - 128, 128)]
                ltiles = [(H + tt * 512, 512) for tt in range(4)]
                bank_rr = [0]

                def mm_tile(wt, e0, n, pb):
                    def f(e):
                        for kc in range(KC):
                            i = e.matmul(bank(pb, n), wt[:, kc, :], hT[:, kc, e0:e0 + n], start=(kc == 0),
                                         stop=(kc == KC - 1))
                        return i
                    fw.op("pe", f, reads=[wt], writes=[PB(pb)])

                def next_bank(lo=0, cnt=4):
                    b = lo + bank_rr[0] % cnt
                    bank_rr[0] += 1
                    return b

                ws = WStream(L1, [128, KC, 128], nbuf=4)

                with ExitStack() as ph:
                    stg = [fw.sbuf(ph, "gstg", [128, T], BF16) for _ in range(2)]

                    def mk_gl(n):
                        def comp(wts):
                            s_t = stg[n % 2]
                            for (e0, nn) in ltiles:
                                pb = next_bank()
                                mm_tile(wts[0], e0, nn, pb)
                                fw.op("act", lambda e: e.activation(s_t[:, e0 - H:e0 - H + nn], bank(pb, nn),
                                                                    AF.Gelu_apprx_tanh),
                                      reads=[PB(pb)], writes=[s_t])
                            fw.dma("sp", ggT.ap()[n], s_t[:], s_t, reads=[s_t])
                        return comp
                    run_jobs([([w_in_t.ap()[l, CGL + n]], mk_gl(n)) for n in range(16)], ws)
                    fw.barrier()

                with ExitStack() as ph:
                    lrw = fw.sbuf(ph, "lrw", [128, 2, 2, 16, 128], BF16)
                    fw.dma("pool", lrw[:], lrw_t.ap()[l], lrw, writes=[lrw])
                    xlT = fw.sbuf(ph, "xlT", [128, E], BF16)
                    dg = [fw.sbuf(ph, "dg", [128, 4, 128], BF16) for _ in range(2)]
                    xcf = fw.sbuf(ph, "xcf", [128, T], F32)
                    xcb = fw.sbuf(ph, "xcb", [128, T], BF16)
                    thr = fw.sbuf(ph, "thr", [128, T], F32)
                    thi = fw.sbuf(ph, "thi", [128, T], F32)
                    a_ts = [fw.sbuf(ph, "a_t", [128, T], F32) for _ in range(2)]
                    a2 = fw.sbuf(ph, "a2", [128, T], F32)
                    b_ts = [fw.sbuf(ph, "b_t", [128, T], F32) for _ in range(2)]
                    st1 = fw.sbuf(ph, "st1", [128, 32], F32)
                    dgi = [0]

                    pend = [None]

                    def mk_xl(n):
                        def comp(wts):
                            for (e0, nn) in etiles:
                                pb = next_bank()
                                mm_tile(wts[0], e0, nn, pb)
                                fw.op("act", lambda e: e.activation(xlT[:, e0:e0 + nn], bank(pb, nn), AF.Copy),
                                      reads=[PB(pb)], writes=[(xlT, e0)])
                            xl_keys = [(xlT, e0) for (e0, _) in etiles]
                            thr_k = [(thr, ti) for ti in range(4)]
                            thi_k = [(thi, ti) for ti in range(4)]
                            xcf_k = [(xcf, ti) for ti in range(4)]
                            for d in range(2):
                                dn = d * 16 + n
                                it = 2 * n + d
                                a_t, b_t = a_ts[it % 2], b_ts[it % 2]
                                dg_t = dg[dgi[0] % 2]
                                dgi[0] += 1
                                for k in range(4):
                                    col = PV_OFF["lcw"] + (d * 4 + k) * 16 + n
                                    fw.op("dve", lambda e: e.tensor_scalar(dg_t[:, k, :], ident[:], pv[:, col:col + 1],
                                                                           None, ALU.mult),
                                          reads=[ident], writes=[dg_t])
                                for ti, (e0, nn) in enumerate(ltiles):
                                    t0 = e0 - H
                                    pb = 4 + (ti % 2)
                                    def fconv(e):
                                        for k in range(4):
                                            sh = (k - 3) if d == 0 else k
                                            i = e.matmul(bank(pb), dg_t[:, k, :], xlT[:, e0 + sh:e0 + sh + nn],
                                                         start=(k == 0), stop=(k == 3))
                                        return i
                                    fw.op("pe", fconv, reads=[dg_t] + xl_keys, writes=[PB(pb)])
                                    cb = PV_OFF["lcb"] + dn
                                    fw.op("act", lambda e: e.activation(xcf[:, t0:t0 + nn], bank(pb), AF.Identity,
                                                                        bias=pv[:, cb:cb + 1], scale=1.0),
                                          reads=[PB(pb)], writes=[(xcf, ti)])
                                    fw.op("dve", lambda e: e.tensor_copy(xcb[:, t0:t0 + nn], xcf[:, t0:t0 + nn]),
                                          reads=[(xcf, ti)], writes=[(xcb, ti)])
                                    pr, pi = 6, 7
                                    fw.op("pe", lambda e: e.matmul(bank(pr), lrw[:, d, 0, n, :], xcb[:, t0:t0 + nn],
                                                                   start=True, stop=True),
                                          reads=[(xcb, ti), lrw], writes=[PB(pr)])
                                    fw.op("pe", lambda e: e.matmul(bank(pi), lrw[:, d, 1, n, :], xcb[:, t0:t0 + nn],
                                                                   start=True, stop=True),
                                          reads=[(xcb, ti), lrw], writes=[PB(pi)])
                                    fw.op("act", lambda e: e.activation(thr[:, t0:t0 + nn], bank(pr), AF.Tanh,
                                                                        bias=pc[:, PC_HBR + dn:PC_HBR + dn + 1], scale=0.5),
                                          reads=[PB(pr)], writes=[(thr, ti)])
                                    fw.op("act", lambda e: e.activation(thi[:, t0:t0 + nn], bank(pi), AF.Tanh,
                                                                        bias=pc[:, PC_HBI + dn:PC_HBI + dn + 1], scale=0.5),
                                          reads=[PB(pi)], writes=[(thi, ti)])
                                if pend[0] is not None:
                                    pend[0]()
                                    pend[0] = None
                                fw.op("act", lambda e: e.activation(a_t[:], thr[:], AF.Exp,
                                                                    bias=pc[:, PC_HC + dn:PC_HC + dn + 1],
                                                                    scale=pc[:, PC_HC + dn:PC_HC + dn + 1]),
                                      reads=thr_k, writes=[a_t])
                                fw.dma("sp", aT.ap()[d, n], a_t[:], a_t, reads=[a_t])
                                fw.op("act", lambda e: e.activation(a2[:], thr[:], AF.Exp,
                                                                    bias=pc[:, PC_C + dn:PC_C + dn + 1],
                                                                    scale=pc[:, PC_C + dn:PC_C + dn + 1]),
                                      reads=thr_k, writes=[a2])
                                fw.op("act", lambda e: e.activation(a2[:], a2[:], AF.Sqrt, bias=1.0, scale=-1.0),
                                      reads=[a2], writes=[a2])
                                fw.op("dve", lambda e: e.scalar_tensor_tensor(b_t[:], thi[:], 1.0, xcf[:], ALU.add, ALU.mult),
                                      reads=thi_k + xcf_k, writes=[b_t])

                                def backB(a_t=a_t, b_t=b_t, d=d, dn=dn, n=n):
                                    fw.op("dve", lambda e: e.scalar_tensor_tensor(b_t[:], a2[:], 0.5, b_t[:], ALU.mult, ALU.mult),
                                          reads=[a2, b_t], writes=[b_t])
                                    fw.dma("sp", bT.ap()[d, n], b_t[:], b_t, reads=[b_t])
                                    if d == 0:
                                        fw.op("dve", lambda e: e.tensor_tensor_scan(a2[:], a_t[:], b_t[:], 0.0, ALU.mult, ALU.add),
                                              reads=[a_t, b_t, a2], writes=[a2])
                                        fw.op("dve", lambda e: e.tensor_copy(st1[:, dn:dn + 1], a2[:, T - 1:T]),
                                              reads=[a2], writes=[st1])
                                    else:
                                        fw.op("dve", lambda e: e.tensor_tensor_scan(a2[:, ::-1], a_t[:, ::-1], b_t[:, ::-1],
                                                                                    0.0, ALU.mult, ALU.add),
                                              reads=[a_t, b_t, a2], writes=[a2])
                                        fw.op("dve", lambda e: e.tensor_copy(st1[:, dn:dn + 1], a2[:, 0:1]),
                                              reads=[a2], writes=[st1])
                                pend[0] = backB
                        return comp
                    run_jobs([([w_in_t.ap()[l, CXL + n]], mk_xl(n)) for n in range(16)], ws)
                    pend[0]()
                    pend[0] = None
                    fw.dma("pool", sx_in.ap(), st1[:], st1, reads=[st1])
                    collective(sx_in, sx_out)

                with ExitStack() as ph:
                    kT = fw.sbuf(ph, "kT", [128, 4, E], BF16)
                    Vt = fw.sbuf(ph, "Vt", [128, 18, 512], BF16)
                    pv2 = ExitStack()
                    wv = fw.sbuf(pv2, "wv", [128, KC, 512], BF16)
                    for c in range(4):
                        fw.dma("pool", wv[:, :, c * 128:(c + 1) * 128], w_in_t.ap()[l, CV + c], wv, writes=[(wv, c)])

                    def mk_k(c):
                        def comp(wts):
                            for (e0, nn) in etiles:
                                pb = next_bank()
                                mm_tile(wts[0], e0, nn, pb)
                                fw.op("act", lambda e: e.activation(kT[:, c, e0:e0 + nn], bank(pb, nn), AF.Copy),
                                      reads=[PB(pb)], writes=[(kT, c, e0)])
                        return comp
                    run_jobs([([w_in_t.ap()[l, CK_ + c]], mk_k(c)) for c in range(4)], ws)
                    for eb in range(18):
                        pb = next_bank()
                        def fv(e):
                            for kc in range(KC):
                                i = e.matmul(bank(pb), hT[:, kc, eb * 128:(eb + 1) * 128], wv[:, kc, :],
                                             start=(kc == 0), stop=(kc == KC - 1))
                            return i
                        fw.op("pe", fv, reads=[(wv, c) for c in range(4)], writes=[PB(pb)])
                        fw.op("dve", lambda e: e.tensor_copy(Vt[:, eb, :], bank(pb)), reads=[PB(pb)], writes=[(Vt, eb)])
                    fw.barrier()
                    pv2.close()
                    abias = fw.sbuf(ph, "abias", [128, 5, 512], BF16)
                    qg = [fw.sbuf(ph, "qg", [128, 4, T], BF16) for _ in range(2)]
                    pt = [fw.sbuf(ph, "pt", [128, 512], BF16) for _ in range(6)]
                    rden = [fw.sbuf(ph, "rden", [128, 512], F32) for _ in range(2)]
                    yst = [fw.sbuf(ph, "yst", [128, 4, T], BF16) for _ in range(1)]
                    pti = [0]
                    SC = 128.0 ** -0.5

                    def attention(kvh, q_t):
                        y_t = yst[0]
                        fw.dma("pool", abias[:], abias_d.ap()[:, :, kvh, :], abias, writes=[abias])
                        for i in range(16):
                            pts = []
                            for j in range(3):
                                kb = i + j
                                var = j
                                if i == 0 and j == 0:
                                    var = 3
                                if i == 15 and j == 2:
                                    var = 4
                                pbs = 4 + (pti[0] % 2)
                                p_t = pt[pti[0] % 6]
                                pti[0] += 1
                                def fs(e):
                                    e.matmul(bank(pbs), ident[:], abias[:, var, :], start=True, stop=False)
                                    for g in range(4):
                                        ins = e.matmul(bank(pbs)[:, g * 128:(g + 1) * 128], kT[:, kvh, kb * 128:(kb + 1) * 128],
                                                       q_t[:, g, i * 128:(i + 1) * 128], start=False, stop=(g == 3))
                                    return ins
                                fw.op("pe", fs, reads=[q_t, abias], writes=[PB(pbs)])
                                fw.op("act", lambda e: e.activation(p_t[:], bank(pbs), AF.Exp, scale=SC),
                                      reads=[PB(pbs)], writes=[p_t])
                                pts.append((p_t, kb))
                            pbo = 6
                            pbd = 7
                            def fo(e):
                                for j, (p_t, kb) in enumerate(pts):
                                    ins = e.matmul(bank(pbo), Vt[:, kb, kvh * 128:(kvh + 1) * 128], p_t[:],
                                                   start=(j == 0), stop=(j == 2))
                                return ins
                            fw.op("pe", fo, reads=[p for p, _ in pts], writes=[PB(pbo)])
                            def fd(e):
                                e.matmul(bank(pbd), ones[0:1, :], esink[0:1, kvh * 512:(kvh + 1) * 512], start=True, stop=False)
                                for j, (p_t, kb) in enumerate(pts):
                                    ins = e.matmul(bank(pbd), ones[:], p_t[:], start=False, stop=(j == 2))
                                return ins
                            fw.op("pe", fd, reads=[p for p, _ in pts] + [esink], writes=[PB(pbd)])
                            r_t = rden[i % 2]
                            fw.op("dve", lambda e: e.reciprocal(r_t[:], bank(pbd)), reads=[PB(pbd)], writes=[r_t])
                            fw.op("dve", lambda e: e.tensor_tensor(
                                y_t[:, :, i * 128:(i + 1) * 128],
                                bank(pbo).rearrange("p (g q) -> p g q", g=4),
                                r_t[:].rearrange("p (g q) -> p g q", g=4), ALU.mult),
                                reads=[PB(pbo), r_t], writes=[(y_t, i)])
                        fw.dma("sp", yaT.ap()[4 * kvh:4 * kvh + 4].rearrange("g p t -> p g t"), y_t[:], y_t,
                               reads=[(y_t, i) for i in range(16)])

                    def mk_q(hh):
                        kvh, g = hh // 4, hh % 4
                        def comp(wts):
                            q_t = qg[kvh % 2]
                            for (e0, nn) in ltiles:
                                pb = next_bank()
                                mm_tile(wts[0], e0, nn, pb)
                                fw.op("act", lambda e: e.activation(q_t[:, g, e0 - H:e0 - H + nn], bank(pb, nn), AF.Copy),
                                      reads=[PB(pb)], writes=[q_t])
                            if g == 3:
                                attention(kvh, q_t)
                        return comp
                    run_jobs([([w_in_t.ap()[l, CQ + hh]], mk_q(hh)) for hh in range(16)], ws)
                    fw.barrier()

                with ExitStack() as ph:
                    stg = [fw.sbuf(ph, "gstg", [128, T], BF16) for _ in range(2)]

                    def mk_g(m):
                        def comp(wts):
                            s_t = stg[m % 2]
                            gb = PV_OFF["gate_bias"] + m
                            for (e0, nn) in ltiles:
                                pb = next_bank()
                                mm_tile(wts[0], e0, nn, pb)
                                fw.op("act", lambda e: e.activation(s_t[:, e0 - H:e0 - H + nn], bank(pb, nn), AF.Sigmoid,
                                                                    bias=pv[:, gb:gb + 1], scale=1.0),
                                      reads=[PB(pb)], writes=[s_t])
                            fw.dma("sp", gT.ap()[m], s_t[:], s_t, reads=[s_t])
                        return comp
                    run_jobs([([w_in_t.ap()[l, CG + m]], mk_g(m)) for m in range(48)], ws)
                    fw.barrier()

                with ExitStack() as ph:
                    gluT = fw.sbuf(ph, "gluT", [128, T + 32], BF16)
                    sg = [fw.sbuf(ph, "sg", [128, 512], F32) for _ in range(2)]
                    dcv = [fw.sbuf(ph, "dcv", [128, CK, 128], BF16) for _ in range(2)]
                    cst = [fw.sbuf(ph, "cst", [128, T], F32) for _ in range(2)]
                    utiles = [(H - 16, 16, 0)] + [(H + tt * 512, 512, 16 + tt * 512) for tt in range(4)] + [(H + T, 16, 16 + T)]

                    def mk_u(n):
                        def comp(wts):
                            d_t = dcv[n % 2]
                            for k in range(CK):
                                col = PV_OFF["dww"] + k * 16 + n
                                fw.op("dve", lambda e: e.tensor_scalar(d_t[:, k, :], ident[:], pv[:, col:col + 1], None,
                                                                       ALU.mult), reads=[ident], writes=[d_t])
                            for ui, (e0, nn, c0) in enumerate(utiles):
                                p1, p2 = 0 + 2 * (ui % 2), 1 + 2 * (ui % 2)
                                mm_tile(wts[0], e0, nn, p1)
                                mm_tile(wts[1], e0, nn, p2)
                                s_t = sg[ui % 2]
                                fw.op("act", lambda e: e.activation(s_t[:, 0:nn], bank(p2, nn), AF.Sigmoid),
                                      reads=[PB(p2)], writes=[s_t])
                                fw.op("dve", lambda e: e.tensor_tensor(gluT[:, c0:c0 + nn], bank(p1, nn), s_t[:, 0:nn], ALU.mult),
                                      reads=[PB(p1), s_t], writes=[(gluT, ui)])
                            c_t = cst[n % 2]
                            gk = [(gluT, ui) for ui in range(6)]
                            for tt in range(4):
                                pb = 4 + (tt % 2)
                                def fc(e):
                                    for k in range(CK):
                                        c0 = 16 + tt * 512 + k - 15
                                        i = e.matmul(bank(pb), d_t[:, k, :], gluT[:, c0:c0 + 512], start=(k == 0),
                                                     stop=(k == CK - 1))
                                    return i
                                fw.op("pe", fc, reads=[d_t] + gk, writes=[PB(pb)])
                                cb = PV_OFF["dwb"] + n
                                fw.op("act", lambda e: e.activation(c_t[:, tt * 512:(tt + 1) * 512], bank(pb), AF.Identity,
                                                                    bias=pv[:, cb:cb + 1], scale=1.0),
                                      reads=[PB(pb)], writes=[c_t])
                            fw.dma("sp", cT.ap()[n], c_t[:], c_t, reads=[c_t])
                        return comp
                    run_jobs([([w_in_t.ap()[l, CU1 + n], w_in_t.ap()[l, CU2 + n]], mk_u(n)) for n in range(16)], ws)
                    fw.barrier()
            fw.barrier()

            with ExitStack() as ph:
                ct = [fw.sbuf(ph, "ct", [128, KC, 512], F32) for _ in range(2)]
                sq = [fw.sbuf(ph, "lsq", [128, 512], BF16) for _ in range(3)]
                cb16 = [fw.sbuf(ph, "cb16", [128, 512], BF16) for _ in range(3)]
                mean = fw.sbuf(ph, "mean", [128, 512], F32)
                msq = fw.sbuf(ph, "msq", [128, 512], F32)
                var = fw.sbuf(ph, "var", [128, 512], F32)
                rs = fw.sbuf(ph, "rs", [128, 512], F32)
                mr = fw.sbuf(ph, "mr", [128, 512], F32)
                tmp = [fw.sbuf(ph, "ltmp", [128, 512], F32) for _ in range(2)]
                yo = [fw.sbuf(ph, "yo", [128, KC, 512], BF16) for _ in range(2)]
                qi = 0
                def ln_load(tt):
                    c_t = ct[tt % 2]
                    fw.dma("sp", c_t[:], cT.ap().rearrange("kc p t -> p kc t")[:, :, tt * 512:(tt + 1) * 512], c_t,
                           writes=[c_t])
                ln_load(0)
                for tt in range(4):
                    c_t = ct[tt % 2]
                    if tt + 1 < 4:
                        ln_load(tt + 1)
                    pm, pvb = 4 + 2 * (tt % 2), 5 + 2 * (tt % 2)
                    for kc in range(KC):
                        s_t, b_t16 = sq[qi % 3], cb16[qi % 3]
                        qi += 1
                        fw.op("act", lambda e: e.activation(s_t[:], c_t[:, kc, :], AF.Square), reads=[c_t], writes=[s_t])
                        fw.op("dve", lambda e: e.tensor_copy(b_t16[:], c_t[:, kc, :]), reads=[c_t], writes=[b_t16])
                        fw.op("pe", lambda e: e.matmul(bank(pm), ones[:], b_t16[:], start=(kc == 0), stop=(kc == KC - 1)),
                              reads=[b_t16], writes=[PB(pm)])
                        fw.op("pe", lambda e: e.matmul(bank(pvb), ones[:], s_t[:], start=(kc == 0), stop=(kc == KC - 1)),
                              reads=[s_t], writes=[PB(pvb)])
                    fw.op("dve", lambda e: e.tensor_scalar(mean[:], bank(pm), 1.0 / D, None, ALU.mult), reads=[PB(pm)],
                          writes=[mean])
                    fw.op("dve", lambda e: e.tensor_tensor(msq[:], mean[:], mean[:], ALU.mult), reads=[mean], writes=[msq])
                    fw.op("dve", lambda e: e.scalar_tensor_tensor(var[:], bank(pvb), 1.0 / D, msq[:], ALU.mult, ALU.subtract),
                          reads=[PB(pvb), msq], writes=[var])
                    fw.op("act", lambda e: e.activation(var[:], var[:], AF.Ln, bias=EPS, scale=1.0), reads=[var], writes=[var])
                    fw.op("act", lambda e: e.activation(rs[:], var[:], AF.Exp, scale=-0.5), reads=[var], writes=[rs])
                    fw.op("dve", lambda e: e.tensor_tensor(mr[:], mean[:], rs[:], ALU.mult), reads=[mean, rs], writes=[mr])
                    y_t = yo[tt % 2]
                    for kc in range(KC):
                        t_t = tmp[kc % 2]
                        fw.op("dve", lambda e: e.tensor_tensor(t_t[:], c_t[:, kc, :], rs[:], ALU.mult), reads=[c_t, rs],
                              writes=[t_t])
                        fw.op("dve", lambda e: e.tensor_tensor(t_t[:], t_t[:], mr[:], ALU.subtract), reads=[t_t, mr],
                              writes=[t_t])
                        gc, bc = PV_OFF["lng"] + kc, PV_OFF["lnb"] + kc
                        fw.op("act", lambda e: e.activation(y_t[:, kc, :], t_t[:], AF.Silu, bias=pv[:, bc:bc + 1],
                                                            scale=pv[:, gc:gc + 1]), reads=[t_t], writes=[y_t])
                    fw.dma("sp", ycT.ap().rearrange("kc p t -> p kc t")[:, :, tt * 512:(tt + 1) * 512], y_t[:], y_t,
                           reads=[y_t])
                fw.barrier()

            with ExitStack() as ph:
                gath = fw.sbuf(ph, "gath", [128, 2, 32], F32)
                hin = fw.sbuf(ph, "hin", [128, 32], F32)
                a_l = [fw.sbuf(ph, "a_l", [128, T], F32) for _ in range(4)]
                b_l = [fw.sbuf(ph, "b_l", [128, T], F32) for _ in range(4)]
                hs = [fw.sbuf(ph, "hs", [128, T], F32) for _ in range(2)]
                gg = [fw.sbuf(ph, "gg", [128, T], BF16) for _ in range(2)]
                yl = [fw.sbuf(ph, "yl", [128, T], BF16) for _ in range(2)]
                fw.dma("pool", gath[:], sx_out.ap().rearrange("(r p) c -> p r c", p=128), gath, writes=[gath])
                fw.op("dve", lambda e: e.tensor_scalar(hin[:, 0:16], gath[:, 0, 0:16], vmask[:, 0:1], None, ALU.mult),
                      reads=[gath], writes=[hin])
                fw.op("dve", lambda e: e.tensor_scalar(hin[:, 16:32], gath[:, 1, 16:32], vmask[:, 1:2], None, ALU.mult),
                      reads=[gath, hin], writes=[hin])
                def l2_load(n):
                    g_t = gg[n % 2]
                    fw.dma("sp", g_t[:], ggT.ap()[n], g_t, writes=[g_t])
                    for d in range(2):
                        al, bl = a_l[(2 * n + d) % 4], b_l[(2 * n + d) % 4]
                        fw.dma("sp", al[:], aT.ap()[d, n], al, writes=[al])
                        fw.dma("sp", bl[:], bT.ap()[d, n], bl, writes=[bl])
                l2_load(0)
                for n in range(16):
                    g_t = gg[n % 2]
                    if n + 1 < 16:
                        l2_load(n + 1)
                    hts = []
                    for d in range(2):
                        al, bl, h_t = a_l[(2 * n + d) % 4], b_l[(2 * n + d) % 4], hs[d]
                        dn = d * 16 + n
                        if d == 0:
                            fw.op("dve", lambda e: e.tensor_tensor_scan(h_t[:], al[:], bl[:], hin[:, dn:dn + 1], ALU.mult, ALU.add),
                                  reads=[al, bl, hin], writes=[h_t])
                        else:
                            fw.op("dve", lambda e: e.tensor_tensor_scan(h_t[:, ::-1], al[:, ::-1], bl[:, ::-1],
                                                                        hin[:, dn:dn + 1], ALU.mult, ALU.add),
                                  reads=[al, bl, hin], writes=[h_t])
                        hts.append(h_t)
                    fw.op("dve", lambda e: e.tensor_tensor(hts[0][:], hts[0][:], hts[1][:], ALU.add), reads=hts, writes=[hts[0]])
                    y_t = yl[n % 2]
                    fw.op("dve", lambda e: e.tensor_tensor(y_t[:], hts[0][:], g_t[:], ALU.mult), reads=[hts[0], g_t], writes=[y_t])
                    fw.dma("sp", ylT.ap()[n], y_t[:], y_t, reads=[y_t])
                    if "hs" in dbg and l == DBG_L:
                        fw.dma("sp", dbg["hs"].ap()[n], hts[0][:], hts[0], reads=[hts[0]])
                fw.barrier()

            with ExitStack() as ph:
                m32 = fw.sbuf(ph, "m32", [128, KC, 1024], F32)
                mbf = fw.sbuf(ph, "mbf", [128, KC, 1024], BF16)
                ybuf = fw.sbuf(ph, "ybuf", [128, KC, 1024], BF16)
                gt_ = [fw.sbuf(ph, "gt", [128, 1024], BF16) for _ in range(2)]
                tm = [fw.sbuf(ph, "tm", [128, 512], F32) for _ in range(2)]
                xo = [fw.sbuf(ph, "xo", [128, 1024], F32) for _ in range(2)]
                ws2 = WStream(ph, [128, KC, 128], nbuf=4)
                ysrc = [yaT, ylT, ycT]
                for half in range(2):
                    h0 = half * 1024
                    for br in range(3):
                        fw.dma("sp", ybuf[:], ysrc[br].ap().rearrange("kc p t -> p kc t")[:, :, h0:h0 + 1024], ybuf,
                               writes=[ybuf])

                        def mk_pre_p(dch, br=br):
                            def pre():
                                g_t = gt_[dch % 2]
                                fw.dma("sp", g_t[:], gT.ap()[br * 16 + dch, :, h0:h0 + 1024], g_t, writes=[g_t])
                            return pre

                        def mk_p(dch, br=br):
                            def comp(wts):
                                g_t = gt_[dch % 2]
                                for sub in range(2):
                                    pb = next_bank()
                                    def f(e):
                                        for kc in range(KC):
                                            i = e.matmul(bank(pb), wts[0][:, kc, :], ybuf[:, kc, sub * 512:(sub + 1) * 512],
                                                         start=(kc == 0), stop=(kc == KC - 1))
                                        return i
                                    fw.op("pe", f, reads=[wts[0], ybuf], writes=[PB(pb)])
                                    dst = m32[:, dch, sub * 512:(sub + 1) * 512]
                                    gs = g_t[:, sub * 512:(sub + 1) * 512]
                                    if br == 0:
                                        fw.op("dve", lambda e: e.tensor_tensor(dst, bank(pb), gs, ALU.mult),
                                              reads=[PB(pb), g_t], writes=[(m32, dch, sub)])
                                    else:
                                        t_t = tm[sub]
                                        fw.op("dve", lambda e: e.tensor_tensor(t_t[:], bank(pb), gs, ALU.mult),
                                              reads=[PB(pb), g_t], writes=[t_t])
                                        fw.op("dve", lambda e: e.tensor_tensor(dst, dst, t_t[:], ALU.add),
                                              reads=[t_t, (m32, dch, sub)], writes=[(m32, dch, sub)])
                            return comp
                        run_jobs([([wp_t.ap()[l, br, dch]], mk_p(dch), mk_pre_p(dch)) for dch in range(16)], ws2)
                        fw.barrier()
                    for kc in range(KC):
                        fw.op("act", lambda e: e.activation(mbf[:, kc, :], m32[:, kc, :], AF.Copy), writes=[(mbf, kc)])
                        if "merged" in dbg and l == DBG_L:
                            fw.dma("sp", dbg["merged"].ap()[kc, :, h0:h0 + 1024], m32[:, kc, :], m32)
                    fw.barrier()

                    def mk_pre_o(dch):
                        def pre():
                            x_t = xo[dch % 2]
                            if l == 0:
                                src = xin.ap()[dch, :, H + h0:H + h0 + 1024]
                            else:
                                src = xres.ap()[dch, :, h0:h0 + 1024]
                            fw.dma("sp", x_t[:], src, x_t, writes=[x_t])
                        return pre

                    def mk_o(dch):
                        def comp(wts):
                            x_t = xo[dch % 2]
                            for sub in range(2):
                                pb = next_bank()
                                def f(e):
                                    for kc in range(KC):
                                        i = e.matmul(bank(pb), wts[0][:, kc, :], mbf[:, kc, sub * 512:(sub + 1) * 512],
                                                     start=(kc == 0), stop=(kc == KC - 1))
                                    return i
                                fw.op("pe", f, reads=[wts[0]], writes=[PB(pb)])
                                xs = x_t[:, sub * 512:(sub + 1) * 512]
                                fw.op("dve", lambda e: e.tensor_tensor(xs, xs, bank(pb), ALU.add), reads=[PB(pb), x_t],
                                      writes=[x_t])
                            fw.dma("sp", xres.ap()[dch, :, h0:h0 + 1024], x_t[:], x_t, reads=[x_t])
                            if "x1" in dbg and l == DBG_L:
                                fw.dma("sp", dbg["x1"].ap()[dch, :, h0:h0 + 1024], x_t[:], x_t, reads=[x_t])
                        return comp
                    run_jobs([([wout_t.ap()[l, dch]], mk_o(dch), mk_pre_o(dch)) for dch in range(16)], ws2)
                    fw.barrier()

            with ExitStack() as ph:
                hcT = fw.sbuf(ph, "hcT", [128, KC, T], BF16)
                rmsnorm_phase(ph, l, "xres", PV_OFF["norm_cross"], out_tile=hcT, with_halo=False)
                memn = fw.sbuf(ph, "memn", [128, KC, 256], BF16)
                kxT = fw.sbuf(ph, "kxT", [128, 4, 256], BF16)
                vx = fw.sbuf(ph, "vx", [128, 2, 512], BF16)
                wkv_v = fw.sbuf(ph, "wkv_v", [128, KC, 512], BF16)
                qxT = fw.sbuf(ph, "qxT", [128, 4, T], BF16)
                oxT = fw.sbuf(ph, "oxT", [128, 4, T], BF16)
                ws3 = WStream(ph, [128, KC, 128], nbuf=4)
                ws4 = WStream(ph, [128, 4, 128], nbuf=4)
                with ExitStack() as p2:
                    mt = fw.sbuf(p2, "mt", [128, KC, 256], F32)
                    sqm = [fw.sbuf(p2, "sqm", [128, 256], BF16) for _ in range(2)]
                    lnm = fw.sbuf(p2, "lnm", [128, 256], F32)
                    rsm = fw.sbuf(p2, "rsm", [128, 256], F32)
                    fw.dma("sp", mt[:], memT.ap().rearrange("kc p t -> p kc t"), mt, writes=[mt])
                    for kc in range(KC):
                        s_t = sqm[kc % 2]
                        fw.op("act", lambda e: e.activation(s_t[:], mt[:, kc, :], AF.Square), reads=[mt], writes=[s_t])
                        fw.op("pe", lambda e: e.matmul(bank(7, 256), ones[:], s_t[:], start=(kc == 0), stop=(kc == KC - 1)),
                              reads=[s_t], writes=[PB(7)])
                    fw.op("act", lambda e: e.activation(lnm[:], bank(7, 256), AF.Ln, bias=EPS, scale=1.0 / D),
                          reads=[PB(7)], writes=[lnm])
                    fw.op("act", lambda e: e.activation(rsm[:], lnm[:], AF.Exp, scale=-0.5), reads=[lnm], writes=[rsm])
                    for kc in range(KC):
                        gc = PV_OFF["norm_mem"] + kc
                        fw.op("dve", lambda e: e.scalar_tensor_tensor(memn[:, kc, :], mt[:, kc, :], pv[:, gc:gc + 1], rsm[:],
                                                                      ALU.mult, ALU.mult), reads=[mt, rsm], writes=[memn])
                    fw.barrier()
                for c in range(4):
                    fw.dma("pool", wkv_v[:, :, c * 128:(c + 1) * 128], xkv_t.ap()[l, 4 + c], wkv_v, writes=[(wkv_v, c)])

                def mk_kx(c):
                    def comp(wts):
                        pb = next_bank()
                        def f(e):
                            for kc in range(KC):
                                i = e.matmul(bank(pb, 256), wts[0][:, kc, :], memn[:, kc, :], start=(kc == 0), stop=(kc == KC - 1))
                            return i
                        fw.op("pe", f, reads=[wts[0]], writes=[PB(pb)])
                        fw.op("act", lambda e: e.activation(kxT[:, c, :], bank(pb, 256), AF.Copy), reads=[PB(pb)],
                              writes=[(kxT, c)])
                    return comp
                run_jobs([([xkv_t.ap()[l, c]], mk_kx(c)) for c in range(4)], ws3)
                for mb in range(2):
                    pb = next_bank()
                    def fvx(e):
                        for kc in range(KC):
                            i = e.matmul(bank(pb), memn[:, kc, mb * 128:(mb + 1) * 128], wkv_v[:, kc, :], start=(kc == 0),
                                         stop=(kc == KC - 1))
                        return i
                    fw.op("pe", fvx, reads=[(wkv_v, c) for c in range(4)], writes=[PB(pb)])
                    fw.op("dve", lambda e: e.tensor_copy(vx[:, mb, :], bank(pb)), reads=[PB(pb)], writes=[(vx, mb)])

                def mk_qx(c):
                    def comp(wts):
                        for tt in range(4):
                            pb = next_bank()
                            def f(e):
                                for kc in range(KC):
                                    i = e.matmul(bank(pb), wts[0][:, kc, :], hcT[:, kc, tt * 512:(tt + 1) * 512],
                                                 start=(kc == 0), stop=(kc == KC - 1))
                                return i
                            fw.op("pe", f, reads=[wts[0]], writes=[PB(pb)])
                            fw.op("act", lambda e: e.activation(qxT[:, c, tt * 512:(tt + 1) * 512], bank(pb), AF.Copy),
                                  reads=[PB(pb)], writes=[(qxT, c, tt)])
                    return comp
                run_jobs([([xq_t.ap()[l, c]], mk_qx(c)) for c in range(4)], ws3)
                fw.barrier()
                ptx = [fw.sbuf(ph, "ptx", [128, 512], BF16) for _ in range(4)]
                rdx = [fw.sbuf(ph, "rdx", [128, 512], F32) for _ in range(2)]
                SCX = 128.0 ** -0.5
                it = 0
                for hh in range(4):
                    for tt in range(4):
                        pts = []
                        for mb in range(2):
                            pbs = 4 + (it % 2)
                            p_t = ptx[it % 4]
                            it += 1
                            fw.op("pe", lambda e: e.matmul(bank(pbs), kxT[:, hh, mb * 128:(mb + 1) * 128],
                                                           qxT[:, hh, tt * 512:(tt + 1) * 512], start=True, stop=True),
                                  writes=[PB(pbs)])
                            fw.op("act", lambda e: e.activation(p_t[:], bank(pbs), AF.Exp, scale=SCX), reads=[PB(pbs)],
                                  writes=[p_t])
                            pts.append(p_t)
                        def fo(e):
                            for mb, p_t in enumerate(pts):
                                i = e.matmul(bank(6), vx[:, mb, hh * 128:(hh + 1) * 128], p_t[:], start=(mb == 0), stop=(mb == 1))
                            return i
                        fw.op("pe", fo, reads=pts, writes=[PB(6)])
                        def fd(e):
                            for mb, p_t in enumerate(pts):
                                i = e.matmul(bank(7), ones[:], p_t[:], start=(mb == 0), stop=(mb == 1))
                            return i
                        fw.op("pe", fd, reads=pts, writes=[PB(7)])
                        r_t = rdx[tt % 2]
                        fw.op("dve", lambda e: e.reciprocal(r_t[:], bank(7)), reads=[PB(7)], writes=[r_t])
                        fw.op("dve", lambda e: e.tensor_tensor(oxT[:, hh, tt * 512:(tt + 1) * 512], bank(6), r_t[:], ALU.mult),
                              reads=[PB(6), r_t], writes=[(oxT, hh, tt)])
                fw.barrier()
                xo = [fw.sbuf(ph, "xo2", [128, T], F32) for _ in range(2)]

                def mk_pre_x(xo, dch):
                    def pre():
                        x_t = xo[dch % 2]
                        fw.dma("sp", x_t[:], xres.ap()[dch], x_t, writes=[x_t])
                    return pre

                def mk_wo(dch):
                    def comp(wts):
                        x_t = xo[dch % 2]
                        for tt in range(4):
                            pb = next_bank()
                            def f(e):
                                for kc in range(4):
                                    i = e.matmul(bank(pb), wts[0][:, kc, :], oxT[:, kc, tt * 512:(tt + 1) * 512],
                                                 start=(kc == 0), stop=(kc == 3))
                                return i
                            fw.op("pe", f, reads=[wts[0]], writes=[PB(pb)])
                            xs = x_t[:, tt * 512:(tt + 1) * 512]
                            fw.op("dve", lambda e: e.tensor_tensor(xs, xs, bank(pb), ALU.add), reads=[PB(pb), x_t], writes=[x_t])
                        fw.dma("sp", xres.ap()[dch], x_t[:], x_t, reads=[x_t])
                        if "x2" in dbg and l == DBG_L:
                            fw.dma("sp", dbg["x2"].ap()[dch], x_t[:], x_t, reads=[x_t])
                    return comp
                run_jobs([([xo_t.ap()[l, dch]], mk_wo(dch), mk_pre_x(xo, dch)) for dch in range(16)], ws4)
                fw.barrier()

            with ExitStack() as ph:
                hfT = fw.sbuf(ph, "hfT", [128, KC, T], BF16)
                rmsnorm_phase(ph, l, "xres", PV_OFF["norm_ffn"], out_tile=hfT, with_halo=False)
                actT = fw.sbuf(ph, "actT", [128, 11, T], BF16)
                sl = [fw.sbuf(ph, "sl", [128, 512], F32) for _ in range(2)]
                xo = [fw.sbuf(ph, "xo3", [128, T], F32) for _ in range(2)]
                ws5 = WStream(ph, [128, KC, 128], nbuf=6)
                ws6 = WStream(ph, [128, 11, 128], nbuf=3)
                for qd in range(4):
                    def mk_f(j):
                        def comp(wts):
                            for tt in range(4):
                                p1, p3 = 0 + 2 * (tt % 2), 1 + 2 * (tt % 2)
                                for (w_, pb) in ((wts[0], p1), (wts[1], p3)):
                                    def f(e, w_=w_, pb=pb):
                                        for kc in range(KC):
                                            i = e.matmul(bank(pb), w_[:, kc, :], hfT[:, kc, tt * 512:(tt + 1) * 512],
                                                         start=(kc == 0), stop=(kc == KC - 1))
                                        return i
                                    fw.op("pe", f, reads=[w_], writes=[PB(pb)])
                                s_t = sl[tt % 2]
                                fw.op("act", lambda e: e.activation(s_t[:], bank(p1), AF.Silu), reads=[PB(p1)], writes=[s_t])
                                fw.op("dve", lambda e: e.tensor_tensor(actT[:, j, tt * 512:(tt + 1) * 512], bank(p3), s_t[:],
                                                                       ALU.mult), reads=[PB(p3), s_t], writes=[(actT, j, tt)])
                        return comp
                    run_jobs([([w13_t.ap()[l, qd * 11 + j], w13_t.ap()[l, 44 + qd * 11 + j]], mk_f(j)) for j in range(11)],
                             ws5, depth=2)
                    fw.barrier()

                    def mk_pre_x3(dch):
                        def pre():
                            x_t = xo[dch % 2]
                            fw.dma("sp", x_t[:], xres.ap()[dch], x_t, writes=[x_t])
                        return pre

                    def mk_w2(dch):
                        def comp(wts):
                            x_t = xo[dch % 2]
                            for tt in range(4):
                                pb = 4 + next_bank(0, 4) % 4
                                def f(e):
                                    for kc in range(11):
                                        i = e.matmul(bank(pb), wts[0][:, kc, :], actT[:, kc, tt * 512:(tt + 1) * 512],
                                                     start=(kc == 0), stop=(kc == 10))
                                    return i
                                fw.op("pe", f, reads=[wts[0]], writes=[PB(pb)])
                                xs = x_t[:, tt * 512:(tt + 1) * 512]
                                fw.op("dve", lambda e: e.tensor_tensor(xs, xs, bank(pb), ALU.add), reads=[PB(pb), x_t],
                                      writes=[x_t])
                            fw.dma("sp", xres.ap()[dch], x_t[:], x_t, reads=[x_t])
                        return comp
                    run_jobs([([w2_t.ap()[l, dch, :, qd * 11:(qd + 1) * 11, :]], mk_w2(dch), mk_pre_x3(dch)) for dch in range(16)], ws6)
                    fw.barrier()

            if l + 1 < DEPTH:
                hv = hx_in.ap().rearrange("p (kc s t) -> p kc s t", kc=KC, s=2)
                with ExitStack() as ph:
                    hb = fw.sbuf(ph, "hb", [128, KC, 2, 128], F32)
                    xr = xres.ap().rearrange("kc p t -> p kc t")
                    fw.dma("pool", hb[:, :, 0, :], xr[:, :, 0:128], hb, writes=[(hb, 0)])
                    fw.dma("pool", hb[:, :, 1, :], xr[:, :, T - 128:T], hb, writes=[(hb, 1)])
                    fw.dma("pool", hv, hb[:], hb, reads=[(hb, 0), (hb, 1)])
                    collective(hx_in, hx_out)

        with ExitStack() as ph:
            rmsnorm_phase(ph, 0, "xres", 0, out_dram=yout, gtile=nf)
        fw.barrier()
        print("n_inst", fw.n_inst, flush=True)
    return nc


def _tile_w(w, nchunk_cols=None):
    K, N = w.shape
    return np.ascontiguousarray(w.reshape(K // 128, 128, N // 128, 128).transpose(2, 1, 0, 3))


def _fm(v):
    sh = v.shape
    c = sh[-1] // 128
    a = v.reshape(-1, c, 128)
    return np.ascontiguousarray(a.transpose(2, 0, 1)).reshape(128, -1)


_NC_CACHE = {}


def _alibi_tiles(lv, rv):
    slopes = 2.0 ** (-(8.0 / 16) * np.arange(1, 17, dtype=np.float64))
    s = np.arange(128)[:, None]
    q = np.arange(128)[None, :]
    NEG = -30000.0
    sc = np.sqrt(128.0)
    out = np.zeros((128, 5, 4, 512), np.float32)
    for var in range(5):
        j = {0: 0, 1: 1, 2: 2, 3: 0, 4: 2}[var]
        rel = q - s + (128 if j == 0 else (0 if j == 1 else -128))
        valid = np.abs(rel) <= 128
        if var == 3 and not lv:
            valid = np.zeros_like(valid)
        if var == 4 and not rv:
            valid = np.zeros_like(valid)
        for kvh in range(4):
            for g in range(4):
                sl = slopes[kvh * 4 + g]
                b = np.where(valid, -sl * np.abs(rel) * sc, NEG * sc)
                out[:, var, kvh, g * 128:(g + 1) * 128] = b
    return out


def kernel(x, mem, norm_mix, w_in, gate_bias, attn_sink, lru_conv_w, lru_conv_b, lru_wr, lru_br,
           lru_wi, lru_bi, lru_lambda, conv_dw_w, conv_dw_b, conv_ln_g, conv_ln_b, w_proj_attn,
           w_proj_lru, w_proj_conv, w_out, norm_cross, norm_mem, xattn_wq, xattn_wkv, xattn_wo,
           norm_ffn, ffn_w13, ffn_w2, norm_final):
    import time as _time
    _t0 = _time.time()
    f = np.float32
    x = np.asarray(x, f)
    mem = np.asarray(mem, f)
    shared = {}
    shared["w_in_t"] = np.stack([_tile_w(np.asarray(w_in[l], f)) for l in range(DEPTH)])
    lrw = np.stack([np.asarray(lru_wr, f), np.asarray(lru_wi, f)], axis=2)
    shared["lrw_t"] = np.ascontiguousarray(lrw.transpose(0, 4, 1, 2, 3, 5))
    shared["wp_t"] = np.stack([np.stack([_tile_w(np.asarray(w[l], f)) for w in (w_proj_attn, w_proj_lru, w_proj_conv)])
                               for l in range(DEPTH)])
    shared["wout_t"] = np.stack([_tile_w(np.asarray(w_out[l], f)) for l in range(DEPTH)])
    shared["xq_t"] = np.stack([_tile_w(np.asarray(xattn_wq[l], f)) for l in range(DEPTH)])
    shared["xkv_t"] = np.stack([_tile_w(np.asarray(xattn_wkv[l], f)) for l in range(DEPTH)])
    shared["xo_t"] = np.stack([_tile_w(np.asarray(xattn_wo[l], f)) for l in range(DEPTH)])
    shared["w13_t"] = np.stack([_tile_w(np.asarray(ffn_w13[l], f)) for l in range(DEPTH)])
    shared["w2_t"] = np.stack([_tile_w(np.asarray(ffn_w2[l], f)) for l in range(DEPTH)])
    pvs = []
    for l in range(DEPTH):
        cols = [_fm(np.asarray(norm_mix[l], f)), _fm(np.asarray(norm_cross[l], f)), _fm(np.asarray(norm_ffn[l], f)),
                _fm(np.asarray(norm_mem[l], f)), _fm(np.asarray(gate_bias[l], f)),
                _fm(np.asarray(lru_conv_w[l], f)), _fm(np.asarray(lru_conv_b[l], f)), _fm(np.asarray(lru_br[l], f)),
                _fm(np.asarray(lru_bi[l], f)), _fm(np.asarray(lru_lambda[l], f)), _fm(np.asarray(conv_dw_w[l], f)),
                _fm(np.asarray(conv_dw_b[l], f)), _fm(np.asarray(conv_ln_g[l], f)), _fm(np.asarray(conv_ln_b[l], f))]
        pvs.append(np.concatenate(cols, axis=1))
    shared["pv"] = np.ascontiguousarray(np.stack(pvs))
    assert shared["pv"].shape == (DEPTH, 128, PV_N), shared["pv"].shape
    shared["nf"] = _fm(np.asarray(norm_final, f))
    shared["sink"] = np.asarray(attn_sink, f).reshape(DEPTH, 1, 16)
    shared["ident"] = np.eye(128, dtype=f)

    if WSHARD:
        big = {n: shared.pop(n) for n, _, _ in WSPEC}
        wsh_core = [[] for _ in range(NCORES)]
        for l_ in range(DEPTH):
            for g_ in range(3):
                flat = np.concatenate([big[n][l_].reshape(-1, 128) for n, _, gg_ in WSPEC if gg_ == g_], axis=0)
                assert flat.shape[0] == PIECE_ROWS[g_]
                r8 = PIECE_ROWS[g_] // 8
                for c in range(NCORES):
                    wsh_core[c].append(flat[c * r8:(c + 1) * r8])
        wsh_core = [np.ascontiguousarray(np.concatenate(w, axis=0)) for w in wsh_core]
        del big
    in_maps = []
    for c in range(NCORES):
        b, half = c // 2, c % 2
        lv, rv = (half == 1), (half == 0)
        xe = np.zeros((E, D), f)
        lo = half * T - H
        s0, s1 = max(lo, 0), min(lo + E, SEQ)
        xe[s0 - lo:s1 - lo] = x[b, s0:s1]
        m = dict(shared)
        m["xin"] = np.ascontiguousarray(xe.T).reshape(KC, 128, E)
        m["memT"] = np.ascontiguousarray(mem[b].T).reshape(KC, 128, 256)
        m["vmask"] = np.tile(np.array([[float(lv), float(rv)]], f), (128, 1))
        m["abias"] = _alibi_tiles(lv, rv)
        if WSHARD:
            m["wsh"] = wsh_core[c]
        in_maps.append(m)

    key = (DEBUG, STOP, WSHARD)
    if key not in _NC_CACHE:
        _NC_CACHE[key] = build_program()
    nc = _NC_CACHE[key]
    _t1 = _time.time()
    res = run_bass_kernel_spmd(nc, in_maps, core_ids=list(range(NCORES)))
    if os.environ.get("MK_TIMING"):
        print("host prep+build %.1fs, run %.1fs" % (_t1 - _t0, _time.time() - _t1), flush=True)
    out = np.zeros((4, SEQ, D), f)
    for c in range(NCORES):
        b, half = c // 2, c % 2
        yT = res.results[c]["yout"].reshape(D, T)
        out[b, half * T:(half + 1) * T] = yT.T
    if DEBUG:
        kernel.dbg = [{k: v for k, v in r.items() if k.startswith("dbg_")} for r in res.results]
    return out
```

```python
import os
from contextlib import ExitStack
import numpy as np
import concourse.bass as bass
import concourse.mybir as mybir
from concourse.bass_utils import run_bass_kernel_spmd

F32 = mybir.dt.float32
BF16 = mybir.dt.bfloat16
AF = mybir.ActivationFunctionType
ALU = mybir.AluOpType

NCORES = 8
DEPTH = 2
D = 2048
KC = 16
T = 2048
H = 128
E = T + 2 * H
SEQ = 4096
NIN = 17408
FFH = 5632
EPS = 1e-6
LRU_C = 8.0
CK = 31
CQ, CK_, CV, CXL, CGL, CU1, CU2, CG = 0, 16, 20, 24, 40, 56, 72, 88
PV_OFF = {}
_o = 0
for _n, _w in (("norm_mix", 16), ("norm_cross", 16), ("norm_ffn", 16), ("norm_mem", 16), ("gate_bias", 48),
               ("lcw", 128), ("lcb", 32), ("lbr", 32), ("lbi", 32), ("lam", 32), ("dww", 496), ("dwb", 16),
               ("lng", 16), ("lnb", 16)):
    PV_OFF[_n] = _o
    _o += _w
PV_N = _o
WSHARD = os.environ.get("MK_WSHARD", "1") == "1"
WSPEC = [("w_in_t", [136, 128, 16, 128], 0), ("lrw_t", [128, 2, 2, 16, 128], 0),
         ("wp_t", [3, 16, 128, 16, 128], 1), ("wout_t", [16, 128, 16, 128], 1),
         ("xq_t", [4, 128, 16, 128], 1), ("xkv_t", [8, 128, 16, 128], 1),
         ("xo_t", [16, 128, 4, 128], 1), ("w13_t", [88, 128, 16, 128], 2),
         ("w2_t", [16, 128, 44, 128], 2)]
WMAP = {}
PIECE_ROWS = [0, 0, 0]
for _n, _sh, _g in WSPEC:
    _rows = int(np.prod(_sh)) // 128
    WMAP[_n] = (PIECE_ROWS[_g], _sh, _rows, _g)
    PIECE_ROWS[_g] += _rows
assert all(r % 8 == 0 for r in PIECE_ROWS)
WR8 = sum(PIECE_ROWS) * DEPTH // 8
DEBUG = os.environ.get("MK_DEBUG", "")
STOP = os.environ.get("MK_STOP", "")
DBG_L = int(os.environ.get("MK_DBG_LAYER", "0"))


class Tile:
    def __init__(self, fw, t, name, stack):
        self.fw, self.t, self.name, self.stack = fw, t, name, stack
        self.sem = None
        self.last_dma = None

    def __getitem__(self, idx):
        return self.t[idx]

    def get_sem(self):
        if self.sem is None:
            self.sem = self.fw.sem_pool.pop()
            self.stack.callback(self._release)
        return self.sem

    def _release(self):
        self.fw.sem_pool.append(self.sem)
        self.sem = None


class FW:
    ENGS = ("pe", "act", "dve", "pool", "sp")

    def __init__(self, nc, stack):
        self.nc, self.stack = nc, stack
        self.eng = {"pe": nc.tensor, "act": nc.scalar, "dve": nc.vector, "pool": nc.gpsimd, "sp": nc.sync}
        self.esem = {e: stack.enter_context(nc.semaphore("e_" + e)) for e in self.ENGS}
        self.ecount = {e: 0 for e in self.ENGS}
        self.waited = {e: {} for e in self.ENGS}
        self.res = {}
        self.dirty = {}
        self.uid = 0
        self.n_inst = 0
        self.sem_pool = [stack.enter_context(nc.semaphore(f"dq{i}")) for i in range(72)]
        self.semcount = {}

    def sbuf(self, stack, name, shape, dtype):
        self.uid += 1
        nm = f"{name}_{self.uid}"
        return Tile(self, stack.enter_context(self.nc.sbuf_tensor(nm, list(shape), dtype)), nm, stack)

    def psum(self, stack, name, shape, dtype=F32):
        self.uid += 1
        nm = f"{name}_{self.uid}"
        return Tile(self, stack.enter_context(self.nc.psum_tensor(nm, list(shape), dtype)), nm, stack)

    def _deps(self, reads, writes):
        evs = []
        for r in reads:
            st = self.res.get(r)
            if st and st["w"] is not None:
                evs.append(st["w"])
        for w in writes:
            st = self.res.get(w)
            if st:
                if st["w"] is not None:
                    evs.append(st["w"])
                evs.extend(st["r"].values())
        return evs

    def _wait(self, e, evs):
        wd = self.waited[e]
        best = {}
        for sem, val in evs:
            k = sem.name
            if e == "pe" and k == self.esem["pe"].name:
                continue
            if wd.get(k, 0) >= val:
                continue
            if k not in best or best[k][1] < val:
                best[k] = (sem, val)
        for k, (sem, val) in best.items():
            self.eng[e].wait_ge(sem, val)
            wd[k] = val
            self.n_inst += 1

    def _record(self, ev, reads, writes):
        for r in reads:
            st = self.res.setdefault(r, {"w": None, "r": {}})
            st["r"][ev[0].name] = ev
        for w in writes:
            self.res[w] = {"w": ev, "r": {}}

    def op(self, e, fn, reads=(), writes=()):
        self._wait(e, self._deps(reads, writes))
        inst = fn(self.eng[e])
        self.n_inst += 1
        self.ecount[e] += 1
        inst.then_inc(self.esem[e], 1)
        ev = (self.esem[e], self.ecount[e])
        self._record(ev, reads, writes)
        return ev

    def dma(self, q, out, in_, tile, reads=(), writes=()):
        evs = self._deps(reads, writes)
        if tile.last_dma is not None:
            evs.append(tile.last_dma)
        self._wait(q, evs)
        sem = tile.get_sem()
        inst = self.eng[q].dma_start(out=out, in_=in_)
        self.n_inst += 1
        cnt = self.semcount.get(sem.name, 0) + 16
        self.semcount[sem.name] = cnt
        inst.then_inc(sem, 16)
        ev = (sem, cnt)
        tile.last_dma = ev
        self.dirty[sem.name] = ev
        self._record(ev, reads, writes)
        return ev

    def barrier(self):
        evs = [(self.esem[e], self.ecount[e]) for e in self.ENGS if self.ecount[e] > 0]
        evs += list(self.dirty.values())
        for e in self.ENGS:
            self._wait(e, evs)
        self.res = {}
        self.dirty = {}


def build_program():
    nc = bass.Bass("TRN2", target_bir_lowering=False)

    def din(name, shape):
        return nc.dram_tensor(name, list(shape), F32, kind="ExternalInput")

    xin = din("xin", [KC, 128, E])
    memT = din("memT", [KC, 128, 256])
    vmask_d = din("vmask", [128, 2])
    abias_d = din("abias", [128, 5, 4, 512])
    ident_d = din("ident", [128, 128])
    pv_d = din("pv", [DEPTH, 128, PV_N])
    nf_d = din("nf", [128, 16])
    sink_d = din("sink", [DEPTH, 1, 16])
    class _V:
        def __init__(self, a):
            self.a = a

        def ap(self):
            return self.a

    class _LV:
        def __init__(self, aps):
            self.aps = aps

        def __getitem__(self, idx):
            if not isinstance(idx, tuple):
                idx = (idx,)
            a = self.aps[idx[0]]
            return a[idx[1:]] if len(idx) > 1 else a

    if not WSHARD:
        _wt = {n: din(n, [DEPTH] + sh) for n, sh, _ in WSPEC}
    else:
        wsh = din("wsh", [WR8, 128])
        wbn = nc.dram_tensor("wbn", [WR8, 128], F32)
        pieces = [[nc.dram_tensor(f"wfull_{l_}_{g_}", [PIECE_ROWS[g_], 128], F32) for g_ in range(3)]
                  for l_ in range(DEPTH)]
        _wt = {}
        for n, sh, g_ in WSPEC:
            r0, _, rows, _ = WMAP[n]
            names = [f"a{i}" for i in range(len(sh) - 1)]
            pat = "(" + " ".join(names) + ") n -> " + " ".join(names) + " n"
            kw = {names[i]: sh[i] for i in range(len(sh) - 1)}
            _wt[n] = _V(_LV([pieces[l_][g_].ap()[r0:r0 + rows, :].rearrange(pat, **kw) for l_ in range(DEPTH)]))
    w_in_t, lrw_t, wp_t, wout_t = _wt["w_in_t"], _wt["lrw_t"], _wt["wp_t"], _wt["wout_t"]
    xq_t, xkv_t, xo_t, w13_t, w2_t = _wt["xq_t"], _wt["xkv_t"], _wt["xo_t"], _wt["w13_t"], _wt["w2_t"]
    yout = nc.dram_tensor("yout", [KC, 128, T], F32, kind="ExternalOutput")
    dbg = {}
    if DEBUG:
        for nm in DEBUG.split(","):
            dbg[nm] = nc.dram_tensor("dbg_" + nm, [KC, 128, T], F32, kind="ExternalOutput")

    def dscr(name, shape, dt=F32):
        return nc.dram_tensor(name, list(shape), dt)

    xres = dscr("xres", [KC, 128, T])
    ggT = dscr("ggT", [16, 128, T], BF16)
    aT = dscr("aT", [2, 16, 128, T])
    bT = dscr("bT", [2, 16, 128, T])
    yaT = dscr("yaT", [16, 128, T], BF16)
    ylT = dscr("ylT", [16, 128, T], BF16)
    ycT = dscr("ycT", [16, 128, T], BF16)
    gT = dscr("gT", [48, 128, T], BF16)
    cT = dscr("cT", [16, 128, T])
    sx_in = dscr("sx_in", [128, 32])
    sx_out = dscr("sx_out", [256, 32])
    hx_in = dscr("hx_in", [128, KC * 2 * 128])
    hx_out = dscr("hx_out", [256, KC * 2 * 128])

    with ExitStack() as stack:
        fw = FW(nc, stack)
        cc_sem = stack.enter_context(nc.semaphore("cc_sem"))
        cc_count = [0]
        ps = fw.psum(stack, "ps", [128, 4096], F32)
        ident = fw.sbuf(stack, "ident", [128, 128], BF16)
        ones = fw.sbuf(stack, "ones", [128, 128], BF16)
        zrow = fw.sbuf(stack, "zrow", [1, 512], F32)
        vmask = fw.sbuf(stack, "vmask", [128, 2], F32)
        pv = fw.sbuf(stack, "pv", [128, PV_N], F32)
        pc = fw.sbuf(stack, "pc", [128, 192], F32)
        nf = fw.sbuf(stack, "nf", [128, 16], F32)
        sinkt = fw.sbuf(stack, "sinkt", [1, 16], F32)
        esink = fw.sbuf(stack, "esink", [1, 2048], BF16)
        PC_HBR, PC_HBI, PC_C, PC_HC = 0, 32, 64, 96

        def bank(b, n=512):
            return ps[:, b * 512:b * 512 + n]

        def PB(b):
            return ("ps", b)

        fw.dma("pool", ident[:], ident_d.ap(), ident, writes=[ident])
        fw.dma("sp", vmask[:], vmask_d.ap(), vmask, writes=[vmask])
        fw.dma("sp", nf[:], nf_d.ap(), nf, writes=[nf])
        fw.op("dve", lambda e: e.memset(ones[:], 1.0), writes=[ones])
        fw.op("dve", lambda e: e.memset(zrow[:], 0.0), writes=[zrow])
        fw.barrier()
        if WSHARD:
            CH = 4096
            for r0 in range(0, WR8, CH):
                r1 = min(WR8, r0 + CH)
                fw.dma("pool", wbn.ap()[r0:r1, :], wsh.ap()[r0:r1, :], zrow)
            fw.barrier()
            piece_off = {}
            off = 0
            for l_ in range(DEPTH):
                for g_ in range(3):
                    piece_off[(l_, g_)] = off
                    off += PIECE_ROWS[g_] // 8
            piece_cnt = {}

            def issue_piece(l_, g_):
                r8 = PIECE_ROWS[g_] // 8
                o_ = piece_off[(l_, g_)]
                nc.gpsimd.collective_compute("AllGather", ALU.bypass,
                                             replica_groups=[list(range(8))],
                                             ins=[wbn.ap()[o_:o_ + r8, :]],
                                             outs=[pieces[l_][g_].ap()]).then_inc(cc_sem)
                cc_count[0] += 1
                piece_cnt[(l_, g_)] = cc_count[0]

            def need_piece(l_, g_):
                nc.gpsimd.wait_ge(cc_sem, piece_cnt[(l_, g_)])

            issue_piece(0, 0)
            issue_piece(0, 1)
        else:
            def issue_piece(l_, g_):
                pass

            def need_piece(l_, g_):
                pass

        def dbg_store(nm, src_tile, src_ap, kc, t0, n, reads):
            if nm in dbg:
                fw.dma("sp", dbg[nm].ap()[kc, :, t0:t0 + n], src_ap, src_tile, reads=reads)

        def collective(in_h, out_h):
            g = nc.gpsimd
            fw.barrier()
            g.collective_compute("AllGather", ALU.bypass,
                                 replica_groups=[[0, 1], [2, 3], [4, 5], [6, 7]],
                                 ins=[in_h.ap().opt()], outs=[out_h.ap().opt()]).then_inc(cc_sem)
            cc_count[0] += 1
            for e in fw.ENGS:
                fw.eng[e].wait_ge(cc_sem, cc_count[0])

        def rmsnorm_phase(st, l, src_kind, gcol, out_tile=None, out_dram=None, with_halo=False, gtile=None):
            gt = gtile if gtile is not None else pv
            with ExitStack() as ph:
                xt = [fw.sbuf(ph, "xt", [128, KC, 512], F32) for _ in range(2)]
                sq = [fw.sbuf(ph, "sq", [128, 512], BF16) for _ in range(3)]
                lnv = fw.sbuf(ph, "lnv", [128, 512], F32)
                rstd = [fw.sbuf(ph, "rstd", [128, 512], F32) for _ in range(2)]
                ot = [fw.sbuf(ph, "ot", [128, KC, 512], F32) for _ in range(2)] if out_dram is not None else None
                tiles = [("loc", tt * 512, 512) for tt in range(4)]
                if with_halo:
                    tiles.append(("halo", 0, 256))
                sqi = 0
                for ti, (kind, t0, n) in enumerate(tiles):
                    b = ti % 2
                    x_t = xt[b]
                    if kind == "loc":
                        if src_kind == "xin":
                            src = xin.ap().rearrange("kc p t -> p kc t")[:, :, H + t0:H + t0 + n]
                        else:
                            src = xres.ap().rearrange("kc p t -> p kc t")[:, :, t0:t0 + n]
                        fw.dma("sp", x_t[:, :, 0:n], src, x_t, writes=[x_t])
                    else:
                        if src_kind == "xin":
                            fw.dma("sp", x_t[:, :, 0:128], xin.ap().rearrange("kc p t -> p kc t")[:, :, 0:128],
                                   x_t, writes=[x_t])
                            fw.dma("sp", x_t[:, :, 128:256],
                                   xin.ap().rearrange("kc p t -> p kc t")[:, :, E - 128:E], x_t, writes=[x_t])
                        else:
                            hv = hx_out.ap().rearrange("(r p) (kc s t) -> r p kc s t", p=128, kc=KC, s=2)
                            fw.dma("pool", x_t[:, :, 0:128], hv[0, :, :, 1, :], x_t, writes=[x_t])
                            fw.dma("pool", x_t[:, :, 128:256], hv[1, :, :, 0, :], x_t, writes=[x_t])
                        fw.op("dve", lambda e: e.tensor_scalar(x_t[:, :, 0:128], x_t[:, :, 0:128], vmask[:, 0:1], None,
                                                               ALU.mult), reads=[x_t, vmask], writes=[x_t])
                        fw.op("dve", lambda e: e.tensor_scalar(x_t[:, :, 128:256], x_t[:, :, 128:256], vmask[:, 1:2],
                                                               None, ALU.mult), reads=[x_t, vmask], writes=[x_t])
                    pb = 6 + (ti % 2)
                    for kc in range(KC):
                        s_t = sq[sqi % 3]
                        sqi += 1
                        fw.op("act", lambda e: e.activation(s_t[:, 0:n], x_t[:, kc, 0:n], AF.Square),
                              reads=[x_t], writes=[s_t])
                        fw.op("pe", lambda e: e.matmul(bank(pb, n), ones[:], s_t[:, 0:n], start=(kc == 0),
                                                       stop=(kc == KC - 1)), reads=[s_t, ones], writes=[PB(pb)])
                    r_t = rstd[b]
                    fw.op("act", lambda e: e.activation(lnv[:, 0:n], bank(pb, n), AF.Ln, bias=EPS, scale=1.0 / D),
                          reads=[PB(pb)], writes=[lnv])
                    fw.op("act", lambda e: e.activation(r_t[:, 0:n], lnv[:, 0:n], AF.Exp, scale=-0.5),
                          reads=[lnv], writes=[r_t])
                    for kc in range(KC):
                        if out_tile is not None:
                            if kind == "loc":
                                e0 = (H + t0) if with_halo else t0
                                dsts = [(out_tile[:, kc, e0:e0 + n], 0, n)]
                            else:
                                dsts = [(out_tile[:, kc, 0:128], 0, 128), (out_tile[:, kc, E - 128:E], 128, 128)]
                            for dst, c0, cn in dsts:
                                fw.op("dve", lambda e: e.scalar_tensor_tensor(
                                    dst, x_t[:, kc, c0:c0 + cn], gt[:, gcol + kc:gcol + kc + 1], r_t[:, c0:c0 + cn],
                                    ALU.mult, ALU.mult), reads=[x_t, r_t, gt], writes=[(out_tile, ti)])
                        else:
                            o_t = ot[b]
                            fw.op("dve", lambda e: e.scalar_tensor_tensor(
                                o_t[:, kc, 0:n], x_t[:, kc, 0:n], gt[:, gcol + kc:gcol + kc + 1], r_t[:, 0:n],
                                ALU.mult, ALU.mult), reads=[x_t, r_t, gt], writes=[o_t])
                    if out_dram is not None:
                        fw.dma("sp", out_dram.ap().rearrange("kc p t -> p kc t")[:, :, t0:t0 + n], ot[b][:, :, 0:n],
                               ot[b], reads=[ot[b]])
                fw.barrier()

        class WStream:
            def __init__(self, st, shape, nbuf=3):
                self.bufs = [fw.sbuf(st, "wb", shape, BF16) for _ in range(nbuf)]
                self.i = 0

            def load(self, src_ap, sub=None):
                t = self.bufs[self.i % len(self.bufs)]
                self.i += 1
                dst = t[:] if sub is None else sub(t)
                fw.dma("pool", dst, src_ap, t, writes=[t])
                return t

        def run_jobs(jobs, ws, depth=2):
            loaded = {}

            def ld(i):
                loaded[i] = [ws.load(s) for s in jobs[i][0]]

            def pre(i):
                if len(jobs[i]) > 2 and jobs[i][2] is not None:
                    jobs[i][2]()

            for i in range(min(depth, len(jobs))):
                ld(i)
            if jobs:
                pre(0)
            for i in range(len(jobs)):
                if i + 1 < len(jobs):
                    pre(i + 1)
                jobs[i][1](loaded.pop(i))
                if i + depth < len(jobs):
                    ld(i + depth)

        for l in range(DEPTH):
            fw.dma("sp", pv[:], pv_d.ap()[l], pv, writes=[pv])
            fw.dma("sp", sinkt[:], sink_d.ap()[l], sinkt, writes=[sinkt])
            o = PV_OFF
            fw.op("dve", lambda e: e.tensor_scalar(pc[:, PC_HBR:PC_HBR + 32], pv[:, o["lbr"]:o["lbr"] + 32], 0.5, None,
                                                   ALU.mult), reads=[pv], writes=[pc])
            fw.op("dve", lambda e: e.tensor_scalar(pc[:, PC_HBI:PC_HBI + 32], pv[:, o["lbi"]:o["lbi"] + 32], 0.5, None,
                                                   ALU.mult), reads=[pv], writes=[pc])
            fw.op("act", lambda e: e.activation(pc[:, PC_C:PC_C + 32], pv[:, o["lam"]:o["lam"] + 32], AF.Exp, scale=-1.0),
                  reads=[pv, pc], writes=[pc])
            fw.op("act", lambda e: e.activation(pc[:, PC_C:PC_C + 32], pc[:, PC_C:PC_C + 32], AF.Ln, bias=1.0, scale=1.0),
                  reads=[pc], writes=[pc])
            fw.op("dve", lambda e: e.tensor_scalar(pc[:, PC_C:PC_C + 32], pc[:, PC_C:PC_C + 32], -LRU_C, None, ALU.mult),
                  reads=[pc], writes=[pc])
            fw.op("dve", lambda e: e.tensor_scalar(pc[:, PC_HC:PC_HC + 32], pc[:, PC_C:PC_C + 32], 0.5, None, ALU.mult),
                  reads=[pc], writes=[pc])
            for h in range(16):
                fw.op("act", lambda e: e.activation(esink[0:1, h * 128:(h + 1) * 128], zrow[0:1, 0:128], AF.Exp,
                                                    bias=sinkt[0:1, h:h + 1], scale=0.0),
                      reads=[sinkt, zrow], writes=[esink])
            fw.barrier()

            with ExitStack() as L1:
                hT = fw.sbuf(L1, "hT", [128, KC, E], BF16)
                rmsnorm_phase(L1, l, "xin" if l == 0 else "xres", PV_OFF["norm_mix"], out_tile=hT, with_halo=True)

                etiles = [(0, 128)] + [(H + tt * 512, 512) for tt in range(4)] + [(E - 128, 128)]
                ltiles = [(H + tt * 512, 512) for tt in range(4)]
                bank_rr = [0]

                def mm_tile(wt, e0, n, pb):
                    def f(e):
                        for kc in range(KC):
                            i = e.matmul(bank(pb, n), wt[:, kc, :], hT[:, kc, e0:e0 + n], start=(kc == 0),
                                         stop=(kc == KC - 1))
                        return i
                    fw.op("pe", f, reads=[wt], writes=[PB(pb)])

                def next_bank(lo=0, cnt=4):
                    b = lo + bank_rr[0] % cnt
                    bank_rr[0] += 1
                    return b

                ws = WStream(L1, [128, KC, 128], nbuf=4)
                need_piece(l, 0)

                with ExitStack() as ph:
                    stg = [fw.sbuf(ph, "gstg", [128, T], BF16) for _ in range(2)]

                    def mk_gl(n):
                        def comp(wts):
                            s_t = stg[n % 2]
                            for (e0, nn) in ltiles:
                                pb = next_bank()
                                mm_tile(wts[0], e0, nn, pb)
                                fw.op("act", lambda e: e.activation(s_t[:, e0 - H:e0 - H + nn], bank(pb, nn),
                                                                    AF.Gelu_apprx_tanh),
                                      reads=[PB(pb)], writes=[s_t])
                            fw.dma("sp", ggT.ap()[n], s_t[:], s_t, reads=[s_t])
                        return comp
                    run_jobs([([w_in_t.ap()[l, CGL + n]], mk_gl(n)) for n in range(16)], ws)
                    fw.barrier()

                with ExitStack() as ph:
                    lrw = fw.sbuf(ph, "lrw", [128, 2, 2, 16, 128], BF16)
                    fw.dma("pool", lrw[:], lrw_t.ap()[l], lrw, writes=[lrw])
                    xlT = fw.sbuf(ph, "xlT", [128, E], BF16)
                    dg = [fw.sbuf(ph, "dg", [128, 4, 128], BF16) for _ in range(2)]
                    xcf = fw.sbuf(ph, "xcf", [128, T], F32)
                    xcb = fw.sbuf(ph, "xcb", [128, T], BF16)
                    thr = fw.sbuf(ph, "thr", [128, T], F32)
                    thi = fw.sbuf(ph, "thi", [128, T], F32)
                    a_ts = [fw.sbuf(ph, "a_t", [128, T], F32) for _ in range(2)]
                    a2 = fw.sbuf(ph, "a2", [128, T], F32)
                    b_ts = [fw.sbuf(ph, "b_t", [128, T], F32) for _ in range(2)]
                    st1 = fw.sbuf(ph, "st1", [128, 32], F32)
                    dgi = [0]

                    pend = [None]

                    def mk_xl(n):
                        def comp(wts):
                            for (e0, nn) in etiles:
                                pb = next_bank()
                                mm_tile(wts[0], e0, nn, pb)
                                fw.op("act", lambda e: e.activation(xlT[:, e0:e0 + nn], bank(pb, nn), AF.Copy),
                                      reads=[PB(pb)], writes=[(xlT, e0)])
                            xl_keys = [(xlT, e0) for (e0, _) in etiles]
                            thr_k = [(thr, ti) for ti in range(4)]
                            thi_k = [(thi, ti) for ti in range(4)]
                            xcf_k = [(xcf, ti) for ti in range(4)]
                            for d in range(2):
                                dn = d * 16 + n
                                it = 2 * n + d
                                a_t, b_t = a_ts[it % 2], b_ts[it % 2]
                                dg_t = dg[dgi[0] % 2]
                                dgi[0] += 1
                                for k in range(4):
                                    col = PV_OFF["lcw"] + (d * 4 + k) * 16 + n
                                    fw.op("dve", lambda e: e.tensor_scalar(dg_t[:, k, :], ident[:], pv[:, col:col + 1],
                                                                           None, ALU.mult),
                                          reads=[ident], writes=[dg_t])
                                for ti, (e0, nn) in enumerate(ltiles):
                                    t0 = e0 - H
                                    pb = 4 + (ti % 2)
                                    def fconv(e):
                                        for k in range(4):
                                            sh = (k - 3) if d == 0 else k
                                            i = e.matmul(bank(pb), dg_t[:, k, :], xlT[:, e0 + sh:e0 + sh + nn],
                                                         start=(k == 0), stop=(k == 3))
                                        return i
                                    fw.op("pe", fconv, reads=[dg_t] + xl_keys, writes=[PB(pb)])
                                    cb = PV_OFF["lcb"] + dn
                                    fw.op("act", lambda e: e.activation(xcf[:, t0:t0 + nn], bank(pb), AF.Identity,
                                                                        bias=pv[:, cb:cb + 1], scale=1.0),
                                          reads=[PB(pb)], writes=[(xcf, ti)])
                                    fw.op("dve", lambda e: e.tensor_copy(xcb[:, t0:t0 + nn], xcf[:, t0:t0 + nn]),
                                          reads=[(xcf, ti)], writes=[(xcb, ti)])
                                    pr, pi = 6, 7
                                    fw.op("pe", lambda e: e.matmul(bank(pr), lrw[:, d, 0, n, :], xcb[:, t0:t0 + nn],
                                                                   start=True, stop=True),
                                          reads=[(xcb, ti), lrw], writes=[PB(pr)])
                                    fw.op("pe", lambda e: e.matmul(bank(pi), lrw[:, d, 1, n, :], xcb[:, t0:t0 + nn],
                                                                   start=True, stop=True),
                                          reads=[(xcb, ti), lrw], writes=[PB(pi)])
                                    fw.op("act", lambda e: e.activation(thr[:, t0:t0 + nn], bank(pr), AF.Tanh,
                                                                        bias=pc[:, PC_HBR + dn:PC_HBR + dn + 1], scale=0.5),
                                          reads=[PB(pr)], writes=[(thr, ti)])
                                    fw.op("act", lambda e: e.activation(thi[:, t0:t0 + nn], bank(pi), AF.Tanh,
                                                                        bias=pc[:, PC_HBI + dn:PC_HBI + dn + 1], scale=0.5),
                                          reads=[PB(pi)], writes=[(thi, ti)])
                                if pend[0] is not None:
                                    pend[0]()
                                    pend[0] = None
                                fw.op("act", lambda e: e.activation(a_t[:], thr[:], AF.Exp,
                                                                    bias=pc[:, PC_HC + dn:PC_HC + dn + 1],
                                                                    scale=pc[:, PC_HC + dn:PC_HC + dn + 1]),
                                      reads=thr_k, writes=[a_t])
                                fw.dma("sp", aT.ap()[d, n], a_t[:], a_t, reads=[a_t])
                                fw.op("act", lambda e: e.activation(a2[:], thr[:], AF.Exp,
                                                                    bias=pc[:, PC_C + dn:PC_C + dn + 1],
                                                                    scale=pc[:, PC_C + dn:PC_C + dn + 1]),
                                      reads=thr_k, writes=[a2])
                                fw.op("act", lambda e: e.activation(a2[:], a2[:], AF.Sqrt, bias=1.0, scale=-1.0),
                                      reads=[a2], writes=[a2])
                                fw.op("dve", lambda e: e.scalar_tensor_tensor(b_t[:], thi[:], 1.0, xcf[:], ALU.add, ALU.mult),
                                      reads=thi_k + xcf_k, writes=[b_t])

                                def backB(a_t=a_t, b_t=b_t, d=d, dn=dn, n=n):
                                    fw.op("dve", lambda e: e.scalar_tensor_tensor(b_t[:], a2[:], 0.5, b_t[:], ALU.mult, ALU.mult),
                                          reads=[a2, b_t], writes=[b_t])
                                    fw.dma("sp", bT.ap()[d, n], b_t[:], b_t, reads=[b_t])
                                    if d == 0:
                                        fw.op("dve", lambda e: e.tensor_tensor_scan(a2[:], a_t[:], b_t[:], 0.0, ALU.mult, ALU.add),
                                              reads=[a_t, b_t, a2], writes=[a2])
                                        fw.op("dve", lambda e: e.tensor_copy(st1[:, dn:dn + 1], a2[:, T - 1:T]),
                                              reads=[a2], writes=[st1])
                                    else:
                                        fw.op("dve", lambda e: e.tensor_tensor_scan(a2[:, ::-1], a_t[:, ::-1], b_t[:, ::-1],
                                                                                    0.0, ALU.mult, ALU.add),
                                              reads=[a_t, b_t, a2], writes=[a2])
                                        fw.op("dve", lambda e: e.tensor_copy(st1[:, dn:dn + 1], a2[:, 0:1]),
                                              reads=[a2], writes=[st1])
                                pend[0] = backB
                        return comp
                    run_jobs([([w_in_t.ap()[l, CXL + n]], mk_xl(n)) for n in range(16)], ws)
                    pend[0]()
                    pend[0] = None
                    fw.dma("pool", sx_in.ap(), st1[:], st1, reads=[st1])
                    collective(sx_in, sx_out)
                    if l == 0:
                        issue_piece(0, 2)
                        for l2_ in range(1, DEPTH):
                            for g2_ in range(3):
                                issue_piece(l2_, g2_)

                with ExitStack() as ph:
                    kT = fw.sbuf(ph, "kT", [128, 4, E], BF16)
                    Vt = fw.sbuf(ph, "Vt", [128, 18, 512], BF16)
                    pv2 = ExitStack()
                    wv = fw.sbuf(pv2, "wv", [128, KC, 512], BF16)
                    for c in range(4):
                        fw.dma("pool", wv[:, :, c * 128:(c + 1) * 128], w_in_t.ap()[l, CV + c], wv, writes=[(wv, c)])

                    def mk_k(c):
                        def comp(wts):
                            for (e0, nn) in etiles:
                                pb = next_bank()
                                mm_tile(wts[0], e0, nn, pb)
                                fw.op("act", lambda e: e.activation(kT[:, c, e0:e0 + nn], bank(pb, nn), AF.Copy),
                                      reads=[PB(pb)], writes=[(kT, c, e0)])
                        return comp
                    run_jobs([([w_in_t.ap()[l, CK_ + c]], mk_k(c)) for c in range(4)], ws)
                    for eb in range(18):
                        pb = next_bank()
                        def fv(e):
                            for kc in range(KC):
                                i = e.matmul(bank(pb), hT[:, kc, eb * 128:(eb + 1) * 128], wv[:, kc, :],
                                             start=(kc == 0), stop=(kc == KC - 1))
                            return i
                        fw.op("pe", fv, reads=[(wv, c) for c in range(4)], writes=[PB(pb)])
                        fw.op("dve", lambda e: e.tensor_copy(Vt[:, eb, :], bank(pb)), reads=[PB(pb)], writes=[(Vt, eb)])
                    fw.barrier()
                    pv2.close()
                    abias = fw.sbuf(ph, "abias", [128, 5, 512], BF16)
                    qg = [fw.sbuf(ph, "qg", [128, 4, T], BF16) for _ in range(2)]
                    pt = [fw.sbuf(ph, "pt", [128, 512], BF16) for _ in range(6)]
                    rden = [fw.sbuf(ph, "rden", [128, 512], F32) for _ in range(2)]
                    yst = [fw.sbuf(ph, "yst", [128, 4, T], BF16) for _ in range(1)]
                    pti = [0]
                    SC = 128.0 ** -0.5

                    def attention(kvh, q_t):
                        y_t = yst[0]
                        fw.dma("pool", abias[:], abias_d.ap()[:, :, kvh, :], abias, writes=[abias])
                        for i in range(16):
                            pts = []
                            for j in range(3):
                                kb = i + j
                                var = j
                                if i == 0 and j == 0:
                                    var = 3
                                if i == 15 and j == 2:
                                    var = 4
                                pbs = 4 + (pti[0] % 2)
                                p_t = pt[pti[0] % 6]
                                pti[0] += 1
                                def fs(e):
                                    e.matmul(bank(pbs), ident[:], abias[:, var, :], start=True, stop=False)
                                    for g in range(4):
                                        ins = e.matmul(bank(pbs)[:, g * 128:(g + 1) * 128], kT[:, kvh, kb * 128:(kb + 1) * 128],
                                                       q_t[:, g, i * 128:(i + 1) * 128], start=False, stop=(g == 3))
                                    return ins
                                fw.op("pe", fs, reads=[q_t, abias], writes=[PB(pbs)])
                                fw.op("act", lambda e: e.activation(p_t[:], bank(pbs), AF.Exp, scale=SC),
                                      reads=[PB(pbs)], writes=[p_t])
                                pts.append((p_t, kb))
                            pbo = 6
                            pbd = 7
                            def fo(e):
                                for j, (p_t, kb) in enumerate(pts):
                                    ins = e.matmul(bank(pbo), Vt[:, kb, kvh * 128:(kvh + 1) * 128], p_t[:],
                                                   start=(j == 0), stop=(j == 2))
                                return ins
                            fw.op("pe", fo, reads=[p for p, _ in pts], writes=[PB(pbo)])
                            def fd(e):
                                e.matmul(bank(pbd), ones[0:1, :], esink[0:1, kvh * 512:(kvh + 1) * 512], start=True, stop=False)
                                for j, (p_t, kb) in enumerate(pts):
                                    ins = e.matmul(bank(pbd), ones[:], p_t[:], start=False, stop=(j == 2))
                                return ins
                            fw.op("pe", fd, reads=[p for p, _ in pts] + [esink], writes=[PB(pbd)])
                            r_t = rden[i % 2]
                            fw.op("dve", lambda e: e.reciprocal(r_t[:], bank(pbd)), reads=[PB(pbd)], writes=[r_t])
                            fw.op("dve", lambda e: e.tensor_tensor(
                                y_t[:, :, i * 128:(i + 1) * 128],
                                bank(pbo).rearrange("p (g q) -> p g q", g=4),
                                r_t[:].rearrange("p (g q) -> p g q", g=4), ALU.mult),
                                reads=[PB(pbo), r_t], writes=[(y_t, i)])
                        fw.dma("sp", yaT.ap()[4 * kvh:4 * kvh + 4].rearrange("g p t -> p g t"), y_t[:], y_t,
                               reads=[(y_t, i) for i in range(16)])

                    def mk_q(hh):
                        kvh, g = hh // 4, hh % 4
                        def comp(wts):
                            q_t = qg[kvh % 2]
                            for (e0, nn) in ltiles:
                                pb = next_bank()
                                mm_tile(wts[0], e0, nn, pb)
                                fw.op("act", lambda e: e.activation(q_t[:, g, e0 - H:e0 - H + nn], bank(pb, nn), AF.Copy),
                                      reads=[PB(pb)], writes=[q_t])
                            if g == 3:
                                attention(kvh, q_t)
                        return comp
                    run_jobs([([w_in_t.ap()[l, CQ + hh]], mk_q(hh)) for hh in range(16)], ws)
                    fw.barrier()

                with ExitStack() as ph:
                    stg = [fw.sbuf(ph, "gstg", [128, T], BF16) for _ in range(2)]

                    def mk_g(m):
                        def comp(wts):
                            s_t = stg[m % 2]
                            gb = PV_OFF["gate_bias"] + m
                            for (e0, nn) in ltiles:
                                pb = next_bank()
                                mm_tile(wts[0], e0, nn, pb)
                                fw.op("act", lambda e: e.activation(s_t[:, e0 - H:e0 - H + nn], bank(pb, nn), AF.Sigmoid,
                                                                    bias=pv[:, gb:gb + 1], scale=1.0),
                                      reads=[PB(pb)], writes=[s_t])
                            fw.dma("sp", gT.ap()[m], s_t[:], s_t, reads=[s_t])
                        return comp
                    run_jobs([([w_in_t.ap()[l, CG + m]], mk_g(m)) for m in range(48)], ws)
                    fw.barrier()

                with ExitStack() as ph:
                    gluT = fw.sbuf(ph, "gluT", [128, T + 32], BF16)
                    sg = [fw.sbuf(ph, "sg", [128, 512], F32) for _ in range(2)]
                    dcv = [fw.sbuf(ph, "dcv", [128, CK, 128], BF16) for _ in range(2)]
                    cst = [fw.sbuf(ph, "cst", [128, T], F32) for _ in range(2)]
                    utiles = [(H - 16, 16, 0)] + [(H + tt * 512, 512, 16 + tt * 512) for tt in range(4)] + [(H + T, 16, 16 + T)]

                    def mk_u(n):
                        def comp(wts):
                            d_t = dcv[n % 2]
                            for k in range(CK):
                                col = PV_OFF["dww"] + k * 16 + n
                                fw.op("dve", lambda e: e.tensor_scalar(d_t[:, k, :], ident[:], pv[:, col:col + 1], None,
                                                                       ALU.mult), reads=[ident], writes=[d_t])
                            for ui, (e0, nn, c0) in enumerate(utiles):
                                p1, p2 = 0 + 2 * (ui % 2), 1 + 2 * (ui % 2)
                                mm_tile(wts[0], e0, nn, p1)
                                mm_tile(wts[1], e0, nn, p2)
                                s_t = sg[ui % 2]
                                fw.op("act", lambda e: e.activation(s_t[:, 0:nn], bank(p2, nn), AF.Sigmoid),
                                      reads=[PB(p2)], writes=[s_t])
                                fw.op("dve", lambda e: e.tensor_tensor(gluT[:, c0:c0 + nn], bank(p1, nn), s_t[:, 0:nn], ALU.mult),
                                      reads=[PB(p1), s_t], writes=[(gluT, ui)])
                            c_t = cst[n % 2]
                            gk = [(gluT, ui) for ui in range(6)]
                            for tt in range(4):
                                pb = 4 + (tt % 2)
                                def fc(e):
                                    for k in range(CK):
                                        c0 = 16 + tt * 512 + k - 15
                                        i = e.matmul(bank(pb), d_t[:, k, :], gluT[:, c0:c0 + 512], start=(k == 0),
                                                     stop=(k == CK - 1))
                                    return i
                                fw.op("pe", fc, reads=[d_t] + gk, writes=[PB(pb)])
                                cb = PV_OFF["dwb"] + n
                                fw.op("act", lambda e: e.activation(c_t[:, tt * 512:(tt + 1) * 512], bank(pb), AF.Identity,
                                                                    bias=pv[:, cb:cb + 1], scale=1.0),
                                      reads=[PB(pb)], writes=[c_t])
                            fw.dma("sp", cT.ap()[n], c_t[:], c_t, reads=[c_t])
                        return comp
                    run_jobs([([w_in_t.ap()[l, CU1 + n], w_in_t.ap()[l, CU2 + n]], mk_u(n)) for n in range(16)], ws)
                    fw.barrier()
            fw.barrier()

            with ExitStack() as ph:
                ct = [fw.sbuf(ph, "ct", [128, KC, 512], F32) for _ in range(2)]
                sq = [fw.sbuf(ph, "lsq", [128, 512], BF16) for _ in range(3)]
                cb16 = [fw.sbuf(ph, "cb16", [128, 512], BF16) for _ in range(3)]
                mean = fw.sbuf(ph, "mean", [128, 512], F32)
                msq = fw.sbuf(ph, "msq", [128, 512], F32)
                var = fw.sbuf(ph, "var", [128, 512], F32)
                rs = fw.sbuf(ph, "rs", [128, 512], F32)
                mr = fw.sbuf(ph, "mr", [128, 512], F32)
                tmp = [fw.sbuf(ph, "ltmp", [128, 512], F32) for _ in range(2)]
                yo = [fw.sbuf(ph, "yo", [128, KC, 512], BF16) for _ in range(2)]
                qi = 0
                def ln_load(tt):
                    c_t = ct[tt % 2]
                    fw.dma("sp", c_t[:], cT.ap().rearrange("kc p t -> p kc t")[:, :, tt * 512:(tt + 1) * 512], c_t,
                           writes=[c_t])
                ln_load(0)
                for tt in range(4):
                    c_t = ct[tt % 2]
                    if tt + 1 < 4:
                        ln_load(tt + 1)
                    pm, pvb = 4 + 2 * (tt % 2), 5 + 2 * (tt % 2)
                    for kc in range(KC):
                        s_t, b_t16 = sq[qi % 3], cb16[qi % 3]
                        qi += 1
                        fw.op("act", lambda e: e.activation(s_t[:], c_t[:, kc, :], AF.Square), reads=[c_t], writes=[s_t])
                        fw.op("dve", lambda e: e.tensor_copy(b_t16[:], c_t[:, kc, :]), reads=[c_t], writes=[b_t16])
                        fw.op("pe", lambda e: e.matmul(bank(pm), ones[:], b_t16[:], start=(kc == 0), stop=(kc == KC - 1)),
                              reads=[b_t16], writes=[PB(pm)])
                        fw.op("pe", lambda e: e.matmul(bank(pvb), ones[:], s_t[:], start=(kc == 0), stop=(kc == KC - 1)),
                              reads=[s_t], writes=[PB(pvb)])
                    fw.op("dve", lambda e: e.tensor_scalar(mean[:], bank(pm), 1.0 / D, None, ALU.mult), reads=[PB(pm)],
                          writes=[mean])
                    fw.op("dve", lambda e: e.tensor_tensor(msq[:], mean[:], mean[:], ALU.mult), reads=[mean], writes=[msq])
                    fw.op("dve", lambda e: e.scalar_tensor_tensor(var[:], bank(pvb), 1.0 / D, msq[:], ALU.mult, ALU.subtract),
                          reads=[PB(pvb), msq], writes=[var])
                    fw.op("act", lambda e: e.activation(var[:], var[:], AF.Ln, bias=EPS, scale=1.0), reads=[var], writes=[var])
                    fw.op("act", lambda e: e.activation(rs[:], var[:], AF.Exp, scale=-0.5), reads=[var], writes=[rs])
                    fw.op("dve", lambda e: e.tensor_tensor(mr[:], mean[:], rs[:], ALU.mult), reads=[mean, rs], writes=[mr])
                    y_t = yo[tt % 2]
                    for kc in range(KC):
                        t_t = tmp[kc % 2]
                        fw.op("dve", lambda e: e.tensor_tensor(t_t[:], c_t[:, kc, :], rs[:], ALU.mult), reads=[c_t, rs],
                              writes=[t_t])
                        fw.op("dve", lambda e: e.tensor_tensor(t_t[:], t_t[:], mr[:], ALU.subtract), reads=[t_t, mr],
                              writes=[t_t])
                        gc, bc = PV_OFF["lng"] + kc, PV_OFF["lnb"] + kc
                        fw.op("act", lambda e: e.activation(y_t[:, kc, :], t_t[:], AF.Silu, bias=pv[:, bc:bc + 1],
                                                            scale=pv[:, gc:gc + 1]), reads=[t_t], writes=[y_t])
                    fw.dma("sp", ycT.ap().rearrange("kc p t -> p kc t")[:, :, tt * 512:(tt + 1) * 512], y_t[:], y_t,
                           reads=[y_t])
                fw.barrier()

            with ExitStack() as ph:
                gath = fw.sbuf(ph, "gath", [128, 2, 32], F32)
                hin = fw.sbuf(ph, "hin", [128, 32], F32)
                a_l = [fw.sbuf(ph, "a_l", [128, T], F32) for _ in range(4)]
                b_l = [fw.sbuf(ph, "b_l", [128, T], F32) for _ in range(4)]
                hs = [fw.sbuf(ph, "hs", [128, T], F32) for _ in range(2)]
                gg = [fw.sbuf(ph, "gg", [128, T], BF16) for _ in range(2)]
                yl = [fw.sbuf(ph, "yl", [128, T], BF16) for _ in range(2)]
                fw.dma("pool", gath[:], sx_out.ap().rearrange("(r p) c -> p r c", p=128), gath, writes=[gath])
                fw.op("dve", lambda e: e.tensor_scalar(hin[:, 0:16], gath[:, 0, 0:16], vmask[:, 0:1], None, ALU.mult),
                      reads=[gath], writes=[hin])
                fw.op("dve", lambda e: e.tensor_scalar(hin[:, 16:32], gath[:, 1, 16:32], vmask[:, 1:2], None, ALU.mult),
                      reads=[gath, hin], writes=[hin])
                def l2_load(n):
                    g_t = gg[n % 2]
                    fw.dma("sp", g_t[:], ggT.ap()[n], g_t, writes=[g_t])
                    for d in range(2):
                        al, bl = a_l[(2 * n + d) % 4], b_l[(2 * n + d) % 4]
                        fw.dma("sp", al[:], aT.ap()[d, n], al, writes=[al])
                        fw.dma("sp", bl[:], bT.ap()[d, n], bl, writes=[bl])
                l2_load(0)
                for n in range(16):
                    g_t = gg[n % 2]
                    if n + 1 < 16:
                        l2_load(n + 1)
                    hts = []
                    for d in range(2):
                        al, bl, h_t = a_l[(2 * n + d) % 4], b_l[(2 * n + d) % 4], hs[d]
                        dn = d * 16 + n
                        if d == 0:
                            fw.op("dve", lambda e: e.tensor_tensor_scan(h_t[:], al[:], bl[:], hin[:, dn:dn + 1], ALU.mult, ALU.add),
                                  reads=[al, bl, hin], writes=[h_t])
                        else:
                            fw.op("dve", lambda e: e.tensor_tensor_scan(h_t[:, ::-1], al[:, ::-1], bl[:, ::-1],
                                                                        hin[:, dn:dn + 1], ALU.mult, ALU.add),
                                  reads=[al, bl, hin], writes=[h_t])
                        hts.append(h_t)
                    fw.op("dve", lambda e: e.tensor_tensor(hts[0][:], hts[0][:], hts[1][:], ALU.add), reads=hts, writes=[hts[0]])
                    y_t = yl[n % 2]
                    fw.op("dve", lambda e: e.tensor_tensor(y_t[:], hts[0][:], g_t[:], ALU.mult), reads=[hts[0], g_t], writes=[y_t])
                    fw.dma("sp", ylT.ap()[n], y_t[:], y_t, reads=[y_t])
                    if "hs" in dbg and l == DBG_L:
                        fw.dma("sp", dbg["hs"].ap()[n], hts[0][:], hts[0], reads=[hts[0]])
                fw.barrier()

            with ExitStack() as ph:
                m32 = fw.sbuf(ph, "m32", [128, KC, 1024], F32)
                mbf = fw.sbuf(ph, "mbf", [128, KC, 1024], BF16)
                ybuf = fw.sbuf(ph, "ybuf", [128, KC, 1024], BF16)
                gt_ = [fw.sbuf(ph, "gt", [128, 1024], BF16) for _ in range(2)]
                tm = [fw.sbuf(ph, "tm", [128, 512], F32) for _ in range(2)]
                xo = [fw.sbuf(ph, "xo", [128, 1024], F32) for _ in range(2)]
                ws2 = WStream(ph, [128, KC, 128], nbuf=4)
                need_piece(l, 1)
                ysrc = [yaT, ylT, ycT]
                for half in range(2):
                    h0 = half * 1024
                    for br in range(3):
                        fw.dma("sp", ybuf[:], ysrc[br].ap().rearrange("kc p t -> p kc t")[:, :, h0:h0 + 1024], ybuf,
                               writes=[ybuf])

                        def mk_pre_p(dch, br=br):
                            def pre():
                                g_t = gt_[dch % 2]
                                fw.dma("sp", g_t[:], gT.ap()[br * 16 + dch, :, h0:h0 + 1024], g_t, writes=[g_t])
                            return pre

                        def mk_p(dch, br=br):
                            def comp(wts):
                                g_t = gt_[dch % 2]
                                for sub in range(2):
                                    pb = next_bank()
                                    def f(e):
                                        for kc in range(KC):
                                            i = e.matmul(bank(pb), wts[0][:, kc, :], ybuf[:, kc, sub * 512:(sub + 1) * 512],
                                                         start=(kc == 0), stop=(kc == KC - 1))
                                        return i
                                    fw.op("pe", f, reads=[wts[0], ybuf], writes=[PB(pb)])
                                    dst = m32[:, dch, sub * 512:(sub + 1) * 512]
                                    gs = g_t[:, sub * 512:(sub + 1) * 512]
                                    if br == 0:
                                        fw.op("dve", lambda e: e.tensor_tensor(dst, bank(pb), gs, ALU.mult),
                                              reads=[PB(pb), g_t], writes=[(m32, dch, sub)])
                                    else:
                                        t_t = tm[sub]
                                        fw.op("dve", lambda e: e.tensor_tensor(t_t[:], bank(pb), gs, ALU.mult),
                                              reads=[PB(pb), g_t], writes=[t_t])
                                        fw.op("dve", lambda e: e.tensor_tensor(dst, dst, t_t[:], ALU.add),
                                              reads=[t_t, (m32, dch, sub)], writes=[(m32, dch, sub)])
                            return comp
                        run_jobs([([wp_t.ap()[l, br, dch]], mk_p(dch), mk_pre_p(dch)) for dch in range(16)], ws2)
                        fw.barrier()
                    for kc in range(KC):
                        fw.op("act", lambda e: e.activation(mbf[:, kc, :], m32[:, kc, :], AF.Copy), writes=[(mbf, kc)])
                        if "merged" in dbg and l == DBG_L:
                            fw.dma("sp", dbg["merged"].ap()[kc, :, h0:h0 + 1024], m32[:, kc, :], m32)
                    fw.barrier()

                    def mk_pre_o(dch):
                        def pre():
                            x_t = xo[dch % 2]
                            if l == 0:
                                src = xin.ap()[dch, :, H + h0:H + h0 + 1024]
                            else:
                                src = xres.ap()[dch, :, h0:h0 + 1024]
                            fw.dma("sp", x_t[:], src, x_t, writes=[x_t])
                        return pre

                    def mk_o(dch):
                        def comp(wts):
                            x_t = xo[dch % 2]
                            for sub in range(2):
                                pb = next_bank()
                                def f(e):
                                    for kc in range(KC):
                                        i = e.matmul(bank(pb), wts[0][:, kc, :], mbf[:, kc, sub * 512:(sub + 1) * 512],
                                                     start=(kc == 0), stop=(kc == KC - 1))
                                    return i
                                fw.op("pe", f, reads=[wts[0]], writes=[PB(pb)])
                                xs = x_t[:, sub * 512:(sub + 1) * 512]
                                fw.op("dve", lambda e: e.tensor_tensor(xs, xs, bank(pb), ALU.add), reads=[PB(pb), x_t],
                                      writes=[x_t])
                            fw.dma("sp", xres.ap()[dch, :, h0:h0 + 1024], x_t[:], x_t, reads=[x_t])
                            if "x1" in dbg and l == DBG_L:
                                fw.dma("sp", dbg["x1"].ap()[dch, :, h0:h0 + 1024], x_t[:], x_t, reads=[x_t])
                        return comp
                    run_jobs([([wout_t.ap()[l, dch]], mk_o(dch), mk_pre_o(dch)) for dch in range(16)], ws2)
                    fw.barrier()

            with ExitStack() as ph:
                hcT = fw.sbuf(ph, "hcT", [128, KC, T], BF16)
                rmsnorm_phase(ph, l, "xres", PV_OFF["norm_cross"], out_tile=hcT, with_halo=False)
                memn = fw.sbuf(ph, "memn", [128, KC, 256], BF16)
                kxT = fw.sbuf(ph, "kxT", [128, 4, 256], BF16)
                vx = fw.sbuf(ph, "vx", [128, 2, 512], BF16)
                wkv_v = fw.sbuf(ph, "wkv_v", [128, KC, 512], BF16)
                qxT = fw.sbuf(ph, "qxT", [128, 4, T], BF16)
                oxT = fw.sbuf(ph, "oxT", [128, 4, T], BF16)
                ws3 = WStream(ph, [128, KC, 128], nbuf=4)
                ws4 = WStream(ph, [128, 4, 128], nbuf=4)
                with ExitStack() as p2:
                    mt = fw.sbuf(p2, "mt", [128, KC, 256], F32)
                    sqm = [fw.sbuf(p2, "sqm", [128, 256], BF16) for _ in range(2)]
                    lnm = fw.sbuf(p2, "lnm", [128, 256], F32)
                    rsm = fw.sbuf(p2, "rsm", [128, 256], F32)
                    fw.dma("sp", mt[:], memT.ap().rearrange("kc p t -> p kc t"), mt, writes=[mt])
                    for kc in range(KC):
                        s_t = sqm[kc % 2]
                        fw.op("act", lambda e: e.activation(s_t[:], mt[:, kc, :], AF.Square), reads=[mt], writes=[s_t])
                        fw.op("pe", lambda e: e.matmul(bank(7, 256), ones[:], s_t[:], start=(kc == 0), stop=(kc == KC - 1)),
                              reads=[s_t], writes=[PB(7)])
                    fw.op("act", lambda e: e.activation(lnm[:], bank(7, 256), AF.Ln, bias=EPS, scale=1.0 / D),
                          reads=[PB(7)], writes=[lnm])
                    fw.op("act", lambda e: e.activation(rsm[:], lnm[:], AF.Exp, scale=-0.5), reads=[lnm], writes=[rsm])
                    for kc in range(KC):
                        gc = PV_OFF["norm_mem"] + kc
                        fw.op("dve", lambda e: e.scalar_tensor_tensor(memn[:, kc, :], mt[:, kc, :], pv[:, gc:gc + 1], rsm[:],
                                                                      ALU.mult, ALU.mult), reads=[mt, rsm], writes=[memn])
                    fw.barrier()
                for c in range(4):
                    fw.dma("pool", wkv_v[:, :, c * 128:(c + 1) * 128], xkv_t.ap()[l, 4 + c], wkv_v, writes=[(wkv_v, c)])

                def mk_kx(c):
                    def comp(wts):
                        pb = next_bank()
                        def f(e):
                            for kc in range(KC):
                                i = e.matmul(bank(pb, 256), wts[0][:, kc, :], memn[:, kc, :], start=(kc == 0), stop=(kc == KC - 1))
                            return i
                        fw.op("pe", f, reads=[wts[0]], writes=[PB(pb)])
                        fw.op("act", lambda e: e.activation(kxT[:, c, :], bank(pb, 256), AF.Copy), reads=[PB(pb)],
                              writes=[(kxT, c)])
                    return comp
                run_jobs([([xkv_t.ap()[l, c]], mk_kx(c)) for c in range(4)], ws3)
                for mb in range(2):
                    pb = next_bank()
                    def fvx(e):
                        for kc in range(KC):
                            i = e.matmul(bank(pb), memn[:, kc, mb * 128:(mb + 1) * 128], wkv_v[:, kc, :], start=(kc == 0),
                                         stop=(kc == KC - 1))
                        return i
                    fw.op("pe", fvx, reads=[(wkv_v, c) for c in range(4)], writes=[PB(pb)])
                    fw.op("dve", lambda e: e.tensor_copy(vx[:, mb, :], bank(pb)), reads=[PB(pb)], writes=[(vx, mb)])

                def mk_qx(c):
                    def comp(wts):
                        for tt in range(4):
                            pb = next_bank()
                            def f(e):
                                for kc in range(KC):
                                    i = e.matmul(bank(pb), wts[0][:, kc, :], hcT[:, kc, tt * 512:(tt + 1) * 512],
                                                 start=(kc == 0), stop=(kc == KC - 1))
                                return i
                            fw.op("pe", f, reads=[wts[0]], writes=[PB(pb)])
                            fw.op("act", lambda e: e.activation(qxT[:, c, tt * 512:(tt + 1) * 512], bank(pb), AF.Copy),
                                  reads=[PB(pb)], writes=[(qxT, c, tt)])
                    return comp
                run_jobs([([xq_t.ap()[l, c]], mk_qx(c)) for c in range(4)], ws3)
                fw.barrier()
                ptx = [fw.sbuf(ph, "ptx", [128, 512], BF16) for _ in range(4)]
                rdx = [fw.sbuf(ph, "rdx", [128, 512], F32) for _ in range(2)]
                SCX = 128.0 ** -0.5
                it = 0
                for hh in range(4):
                    for tt in range(4):
                        pts = []
                        for mb in range(2):
                            pbs = 4 + (it % 2)
                            p_t = ptx[it % 4]
                            it += 1
                            fw.op("pe", lambda e: e.matmul(bank(pbs), kxT[:, hh, mb * 128:(mb + 1) * 128],
                                                           qxT[:, hh, tt * 512:(tt + 1) * 512], start=True, stop=True),
                                  writes=[PB(pbs)])
                            fw.op("act", lambda e: e.activation(p_t[:], bank(pbs), AF.Exp, scale=SCX), reads=[PB(pbs)],
                                  writes=[p_t])
                            pts.append(p_t)
                        def fo(e):
                            for mb, p_t in enumerate(pts):
                                i = e.matmul(bank(6), vx[:, mb, hh * 128:(hh + 1) * 128], p_t[:], start=(mb == 0), stop=(mb == 1))
                            return i
                        fw.op("pe", fo, reads=pts, writes=[PB(6)])
                        def fd(e):
                            for mb, p_t in enumerate(pts):
                                i = e.matmul(bank(7), ones[:], p_t[:], start=(mb == 0), stop=(mb == 1))
                            return i
                        fw.op("pe", fd, reads=pts, writes=[PB(7)])
                        r_t = rdx[tt % 2]
                        fw.op("dve", lambda e: e.reciprocal(r_t[:], bank(7)), reads=[PB(7)], writes=[r_t])
                        fw.op("dve", lambda e: e.tensor_tensor(oxT[:, hh, tt * 512:(tt + 1) * 512], bank(6), r_t[:], ALU.mult),
                              reads=[PB(6), r_t], writes=[(oxT, hh, tt)])
                fw.barrier()
                xo = [fw.sbuf(ph, "xo2", [128, T], F32) for _ in range(2)]

                def mk_pre_x(xo, dch):
                    def pre():
                        x_t = xo[dch % 2]
                        fw.dma("sp", x_t[:], xres.ap()[dch], x_t, writes=[x_t])
                    return pre

                def mk_wo(dch):
                    def comp(wts):
                        x_t = xo[dch % 2]
                        for tt in range(4):
                            pb = next_bank()
                            def f(e):
                                for kc in range(4):
                                    i = e.matmul(bank(pb), wts[0][:, kc, :], oxT[:, kc, tt * 512:(tt + 1) * 512],
                                                 start=(kc == 0), stop=(kc == 3))
                                return i
                            fw.op("pe", f, reads=[wts[0]], writes=[PB(pb)])
                            xs = x_t[:, tt * 512:(tt + 1) * 512]
                            fw.op("dve", lambda e: e.tensor_tensor(xs, xs, bank(pb), ALU.add), reads=[PB(pb), x_t], writes=[x_t])
                        fw.dma("sp", xres.ap()[dch], x_t[:], x_t, reads=[x_t])
                        if "x2" in dbg and l == DBG_L:
                            fw.dma("sp", dbg["x2"].ap()[dch], x_t[:], x_t, reads=[x_t])
                    return comp
                run_jobs([([xo_t.ap()[l, dch]], mk_wo(dch), mk_pre_x(xo, dch)) for dch in range(16)], ws4)
                fw.barrier()

            with ExitStack() as ph:
                hfT = fw.sbuf(ph, "hfT", [128, KC, T], BF16)
                rmsnorm_phase(ph, l, "xres", PV_OFF["norm_ffn"], out_tile=hfT, with_halo=False)
                actT = fw.sbuf(ph, "actT", [128, 11, T], BF16)
                sl = [fw.sbuf(ph, "sl", [128, 512], F32) for _ in range(2)]
                xo = [fw.sbuf(ph, "xo3", [128, T], F32) for _ in range(2)]
                ws5 = WStream(ph, [128, KC, 128], nbuf=6)
                need_piece(l, 2)
                ws6 = WStream(ph, [128, 11, 128], nbuf=3)
                for qd in range(4):
                    def mk_f(j):
                        def comp(wts):
                            for tt in range(4):
                                p1, p3 = 0 + 2 * (tt % 2), 1 + 2 * (tt % 2)
                                for (w_, pb) in ((wts[0], p1), (wts[1], p3)):
                                    def f(e, w_=w_, pb=pb):
                                        for kc in range(KC):
                                            i = e.matmul(bank(pb), w_[:, kc, :], hfT[:, kc, tt * 512:(tt + 1) * 512],
                                                         start=(kc == 0), stop=(kc == KC - 1))
                                        return i
                                    fw.op("pe", f, reads=[w_], writes=[PB(pb)])
                                s_t = sl[tt % 2]
                                fw.op("act", lambda e: e.activation(s_t[:], bank(p1), AF.Silu), reads=[PB(p1)], writes=[s_t])
                                fw.op("dve", lambda e: e.tensor_tensor(actT[:, j, tt * 512:(tt + 1) * 512], bank(p3), s_t[:],
                                                                       ALU.mult), reads=[PB(p3), s_t], writes=[(actT, j, tt)])
                        return comp
                    run_jobs([([w13_t.ap()[l, qd * 11 + j], w13_t.ap()[l, 44 + qd * 11 + j]], mk_f(j)) for j in range(11)],
                             ws5, depth=2)
                    fw.barrier()

                    def mk_pre_x3(dch):
                        def pre():
                            x_t = xo[dch % 2]
                            fw.dma("sp", x_t[:], xres.ap()[dch], x_t, writes=[x_t])
                        return pre

                    def mk_w2(dch):
                        def comp(wts):
                            x_t = xo[dch % 2]
                            for tt in range(4):
                                pb = 4 + next_bank(0, 4) % 4
                                def f(e):
                                    for kc in range(11):
                                        i = e.matmul(bank(pb), wts[0][:, kc, :], actT[:, kc, tt * 512:(tt + 1) * 512],
                                                     start=(kc == 0), stop=(kc == 10))
                                    return i
                                fw.op("pe", f, reads=[wts[0]], writes=[PB(pb)])
                                xs = x_t[:, tt * 512:(tt + 1) * 512]
                                fw.op("dve", lambda e: e.tensor_tensor(xs, xs, bank(pb), ALU.add), reads=[PB(pb), x_t],
                                      writes=[x_t])
                            fw.dma("sp", xres.ap()[dch], x_t[:], x_t, reads=[x_t])
                        return comp
                    run_jobs([([w2_t.ap()[l, dch, :, qd * 11:(qd + 1) * 11, :]], mk_w2(dch), mk_pre_x3(dch)) for dch in range(16)], ws6)
                    fw.barrier()

            if l + 1 < DEPTH:
                hv = hx_in.ap().rearrange("p (kc s t) -> p kc s t", kc=KC, s=2)
                with ExitStack() as ph:
                    hb = fw.sbuf(ph, "hb", [128, KC, 2, 128], F32)
                    xr = xres.ap().rearrange("kc p t -> p kc t")
                    fw.dma("pool", hb[:, :, 0, :], xr[:, :, 0:128], hb, writes=[(hb, 0)])
                    fw.dma("pool", hb[:, :, 1, :], xr[:, :, T - 128:T], hb, writes=[(hb, 1)])
                    fw.dma("pool", hv, hb[:], hb, reads=[(hb, 0), (hb, 1)])
                    collective(hx_in, hx_out)

        with ExitStack() as ph:
            rmsnorm_phase(ph, 0, "xres", 0, out_dram=yout, gtile=nf)
        fw.barrier()
        print("n_inst", fw.n_inst, flush=True)
    return nc


def _tile_w(w, nchunk_cols=None):
    K, N = w.shape
    return np.ascontiguousarray(w.reshape(K // 128, 128, N // 128, 128).transpose(2, 1, 0, 3))


def _fm(v):
    sh = v.shape
    c = sh[-1] // 128
    a = v.reshape(-1, c, 128)
    return np.ascontiguousarray(a.transpose(2, 0, 1)).reshape(128, -1)


_NC_CACHE = {}


def _alibi_tiles(lv, rv):
    slopes = 2.0 ** (-(8.0 / 16) * np.arange(1, 17, dtype=np.float64))
    s = np.arange(128)[:, None]
    q = np.arange(128)[None, :]
    NEG = -30000.0
    sc = np.sqrt(128.0)
    out = np.zeros((128, 5, 4, 512), np.float32)
    for var in range(5):
        j = {0: 0, 1: 1, 2: 2, 3: 0, 4: 2}[var]
        rel = q - s + (128 if j == 0 else (0 if j == 1 else -128))
        valid = np.abs(rel) <= 128
        if var == 3 and not lv:
            valid = np.zeros_like(valid)
        if var == 4 and not rv:
            valid = np.zeros_like(valid)
        for kvh in range(4):
            for g in range(4):
                sl = slopes[kvh * 4 + g]
                b = np.where(valid, -sl * np.abs(rel) * sc, NEG * sc)
                out[:, var, kvh, g * 128:(g + 1) * 128] = b
    return out


def kernel(x, mem, norm_mix, w_in, gate_bias, attn_sink, lru_conv_w, lru_conv_b, lru_wr, lru_br,
           lru_wi, lru_bi, lru_lambda, conv_dw_w, conv_dw_b, conv_ln_g, conv_ln_b, w_proj_attn,
           w_proj_lru, w_proj_conv, w_out, norm_cross, norm_mem, xattn_wq, xattn_wkv, xattn_wo,
           norm_ffn, ffn_w13, ffn_w2, norm_final):
    import time as _time
    _t0 = _time.time()
    f = np.float32
    x = np.asarray(x, f)
    mem = np.asarray(mem, f)
    shared = {}
    shared["w_in_t"] = np.stack([_tile_w(np.asarray(w_in[l], f)) for l in range(DEPTH)])
    lrw = np.stack([np.asarray(lru_wr, f), np.asarray(lru_wi, f)], axis=2)
    shared["lrw_t"] = np.ascontiguousarray(lrw.transpose(0, 4, 1, 2, 3, 5))
    shared["wp_t"] = np.stack([np.stack([_tile_w(np.asarray(w[l], f)) for w in (w_proj_attn, w_proj_lru, w_proj_conv)])
                               for l in range(DEPTH)])
    shared["wout_t"] = np.stack([_tile_w(np.asarray(w_out[l], f)) for l in range(DEPTH)])
    shared["xq_t"] = np.stack([_tile_w(np.asarray(xattn_wq[l], f)) for l in range(DEPTH)])
    shared["xkv_t"] = np.stack([_tile_w(np.asarray(xattn_wkv[l], f)) for l in range(DEPTH)])
    shared["xo_t"] = np.stack([_tile_w(np.asarray(xattn_wo[l], f)) for l in range(DEPTH)])
    shared["w13_t"] = np.stack([_tile_w(np.asarray(ffn_w13[l], f)) for l in range(DEPTH)])
    shared["w2_t"] = np.stack([_tile_w(np.asarray(ffn_w2[l], f)) for l in range(DEPTH)])
    pvs = []
    for l in range(DEPTH):
        cols = [_fm(np.asarray(norm_mix[l], f)), _fm(np.asarray(norm_cross[l], f)), _fm(np.asarray(norm_ffn[l], f)),
                _fm(np.asarray(norm_mem[l], f)), _fm(np.asarray(gate_bias[l], f)),
                _fm(np.asarray(lru_conv_w[l], f)), _fm(np.asarray(lru_conv_b[l], f)), _fm(np.asarray(lru_br[l], f)),
                _fm(np.asarray(lru_bi[l], f)), _fm(np.asarray(lru_lambda[l], f)), _fm(np.asarray(conv_dw_w[l], f)),
                _fm(np.asarray(conv_dw_b[l], f)), _fm(np.asarray(conv_ln_g[l], f)), _fm(np.asarray(conv_ln_b[l], f))]
        pvs.append(np.concatenate(cols, axis=1))
    shared["pv"] = np.ascontiguousarray(np.stack(pvs))
    assert shared["pv"].shape == (DEPTH, 128, PV_N), shared["pv"].shape
    shared["nf"] = _fm(np.asarray(norm_final, f))
    shared["sink"] = np.asarray(attn_sink, f).reshape(DEPTH, 1, 16)
    shared["ident"] = np.eye(128, dtype=f)

    if WSHARD:
        big = {n: shared.pop(n) for n, _, _ in WSPEC}
        wsh_core = [[] for _ in range(NCORES)]
        for l_ in range(DEPTH):
            for g_ in range(3):
                flat = np.concatenate([big[n][l_].reshape(-1, 128) for n, _, gg_ in WSPEC if gg_ == g_], axis=0)
                assert flat.shape[0] == PIECE_ROWS[g_]
                r8 = PIECE_ROWS[g_] // 8
                for c in range(NCORES):
                    wsh_core[c].append(flat[c * r8:(c + 1) * r8])
        wsh_core = [np.ascontiguousarray(np.concatenate(w, axis=0)) for w in wsh_core]
        del big
    in_maps = []
    for c in range(NCORES):
        b, half = c // 2, c % 2
        lv, rv = (half == 1), (half == 0)
        xe = np.zeros((E, D), f)
        lo = half * T - H
        s0, s1 = max(lo, 0), min(lo + E, SEQ)
        xe[s0 - lo:s1 - lo] = x[b, s0:s1]
        m = dict(shared)
        m["xin"] = np.ascontiguousarray(xe.T).reshape(KC, 128, E)
        m["memT"] = np.ascontiguousarray(mem[b].T).reshape(KC, 128, 256)
        m["vmask"] = np.tile(np.array([[float(lv), float(rv)]], f), (128, 1))
        m["abias"] = _alibi_tiles(lv, rv)
        if WSHARD:
            m["wsh"] = wsh_core[c]
        in_maps.append(m)

    key = (DEBUG, STOP, WSHARD)
    if key not in _NC_CACHE:
        _NC_CACHE[key] = build_program()
    nc = _NC_CACHE[key]
    _t1 = _time.time()
    res = run_bass_kernel_spmd(nc, in_maps, core_ids=list(range(NCORES)))
    if os.environ.get("MK_TIMING"):
        print("host prep+build %.1fs, run %.1fs" % (_t1 - _t0, _time.time() - _t1), flush=True)
    out = np.zeros((4, SEQ, D), f)
    for c in range(NCORES):
        b, half = c // 2, c % 2
        yT = res.results[c]["yout"].reshape(D, T)
        out[b, half * T:(half + 1) * T] = yT.T
    if DEBUG:
        kernel.dbg = [{k: v for k, v in r.items() if k.startswith("dbg_")} for r in res.results]
    return out
```
